# Optimizing a Trainium2 kernel written in Bass

```python
import math
import jax, jax.numpy as jnp
from jax import lax
import numpy as np

D_MODEL = 1024
BATCH = 4
SEQ = 4096
DEPTH = 1

N_HEADS = 8
N_KV_HEADS = 2
GROUP = N_HEADS // N_KV_HEADS
HEAD_DIM = 64
ATTN_WIDTH = N_HEADS * HEAD_DIM
KV_WIDTH = N_KV_HEADS * HEAD_DIM
WINDOW = 128
BLOCK = 128
HYENA_WIDTH = 512
HYENA_ORDER = 2
SHORT_CONV = 3
N_BANDS = 16
FILTER_EMB = 2 * N_BANDS + 1
FILTER_HIDDEN = 64
FILTER_OUT = 2 * HYENA_ORDER * HYENA_WIDTH
FAST_DECAY_PCT = 0.3
SLOW_DECAY_PCT = 1.5
DECAY_TARGET = 1e-2
D_FF = ((-(-8 * D_MODEL // 3) + 255) // 256) * 256
IN_COLS = ATTN_WIDTH + 2 * KV_WIDTH + 3 * HYENA_WIDTH + 2 * D_MODEL
NORM_EPS = 1e-6
MASK_VALUE = -1e30

kernel_name = "hybrid_gated_swa_hyena_encoder_block"


def rms_norm(x, g):
    xf = x.astype(jnp.float32)
    y = xf * lax.rsqrt(jnp.mean(xf * xf, axis=-1, keepdims=True) + NORM_EPS)
    return (y * g.astype(jnp.float32)).astype(x.dtype)


def alibi_slopes():
    h = np.arange(1, N_HEADS + 1, dtype=np.float32)
    return jnp.asarray(2.0 ** (-8.0 * h / N_HEADS), dtype=jnp.float32)


def windowed_attention(q, k, v, q_gain, k_gain, sink):
    B, L, _ = q.shape
    nb = L // BLOCK
    qf = rms_norm(q.reshape(B, L, N_HEADS, HEAD_DIM), q_gain).astype(jnp.float32)
    kf = rms_norm(k.reshape(B, L, N_KV_HEADS, HEAD_DIM), k_gain).astype(jnp.float32)
    vf = v.reshape(B, L, N_KV_HEADS, HEAD_DIM).astype(jnp.float32)
    qb = qf.reshape(B, nb, BLOCK, N_KV_HEADS, GROUP, HEAD_DIM)
    pad = ((0, 0), (BLOCK, BLOCK), (0, 0), (0, 0))
    kp = jnp.pad(kf, pad).reshape(B, nb + 2, BLOCK, N_KV_HEADS, HEAD_DIM)
    vp = jnp.pad(vf, pad).reshape(B, nb + 2, BLOCK, N_KV_HEADS, HEAD_DIM)
    kb = jnp.concatenate([kp[:, :-2], kp[:, 1:-1], kp[:, 2:]], axis=2)
    vb = jnp.concatenate([vp[:, :-2], vp[:, 1:-1], vp[:, 2:]], axis=2)
    scores = jnp.einsum('bnqhgd,bnshd->bnhgqs', qb, kb) * (HEAD_DIM ** -0.5)
    iq = jnp.arange(BLOCK)[:, None]
    js = jnp.arange(3 * BLOCK)[None, :]
    dist = jnp.abs(iq - js + BLOCK)
    s_pos = jnp.arange(nb)[:, None, None] * BLOCK - BLOCK + js[None]
    mask = (dist <= WINDOW)[None] & (s_pos >= 0) & (s_pos < L)
    slopes = alibi_slopes().reshape(N_KV_HEADS, GROUP)
    bias = -slopes[:, :, None, None] * dist.astype(jnp.float32)
    logits = jnp.where(mask[None, :, None, None], scores + bias[None, None], MASK_VALUE)
    sink_col = jnp.broadcast_to(
        sink.astype(jnp.float32).reshape(N_KV_HEADS, GROUP)[None, None, :, :, None, None],
        logits.shape[:-1] + (1,))
    probs = jax.nn.softmax(jnp.concatenate([logits, sink_col], axis=-1), axis=-1)[..., :-1]
    out = jnp.einsum('bnhgqs,bnshd->bnqhgd', probs, vb)
    return out.reshape(B, L, ATTN_WIDTH).astype(q.dtype)


def short_conv(u, w, b):
    up = jnp.pad(u, ((0, 0), (1, 1), (0, 0)))
    return up[:, :-2] * w[0] + up[:, 1:-1] * w[1] + up[:, 2:] * w[2] + b


def implicit_filters(L, w1, b1, f1, w2, b2, f2, w3):
    t = jnp.linspace(0.0, 1.0, L, dtype=jnp.float32)[:, None]
    w = 2.0 * math.pi * jnp.arange(L, dtype=jnp.float32)[:, None] / L
    bands = jnp.linspace(1e-4, N_BANDS - 1, N_BANDS, dtype=jnp.float32)[None, :]
    z = jnp.concatenate([t, jnp.cos(bands * w), -jnp.sin(bands * w)], axis=-1)
    h = jnp.sin(f1.astype(jnp.float32) * (z @ w1.astype(jnp.float32) + b1.astype(jnp.float32)))
    h = jnp.sin(f2.astype(jnp.float32) * (h @ w2.astype(jnp.float32) + b2.astype(jnp.float32)))
    h = (h @ w3.astype(jnp.float32)).reshape(L, 2, HYENA_ORDER, HYENA_WIDTH)
    max_decay = math.log(DECAY_TARGET) / FAST_DECAY_PCT
    min_decay = math.log(DECAY_TARGET) / SLOW_DECAY_PCT
    deltas = jnp.linspace(min_decay, max_decay, HYENA_WIDTH, dtype=jnp.float32)
    decay = jnp.exp(-t * jnp.abs(deltas)[None, :])
    h = h * decay[:, None, None, :]
    h_fwd, h_bwd = h[:, 0], h[:, 1]
    kern = jnp.concatenate([h_fwd, jnp.zeros((1, HYENA_ORDER, HYENA_WIDTH), jnp.float32),
                            jnp.flip(h_bwd[1:], axis=0)], axis=0)
    return kern / (jnp.sum(jnp.abs(kern), axis=0, keepdims=True) + NORM_EPS)


def long_conv(u, kern, d):
    L = u.shape[1]
    uf = u.astype(jnp.float32)
    y = jnp.fft.irfft(jnp.fft.rfft(uf, n=2 * L, axis=1) * jnp.fft.rfft(kern, n=2 * L, axis=0)[None],
                      n=2 * L, axis=1)[:, :L]
    return (y + uf * d.astype(jnp.float32)).astype(u.dtype)


def hyena_mixer(u, conv_w, conv_b, w1, b1, f1, w2, b2, f2, w3, bias_d):
    L = u.shape[1]
    uc = short_conv(u, conv_w, conv_b)
    v, x1, x2 = jnp.split(uc, 3, axis=-1)
    kern = implicit_filters(L, w1, b1, f1, w2, b2, f2, w3)
    z = x1 * long_conv(v, kern[:, 0], bias_d[0])
    return x2 * long_conv(z, kern[:, 1], bias_d[1])


def setup_inputs(seed: int = 0) -> dict:
    key = jax.random.key(seed)
    ks = jax.random.split(key, 24)
    f32 = jnp.float32
    nrm = lambda k, shape, s: jax.random.normal(k, shape, f32) * s
    return {
        "x": nrm(ks[0], (BATCH, SEQ, D_MODEL), 1.0),
        "norm1_g": 1.0 + nrm(ks[1], (D_MODEL,), 0.02),
        "w_in": nrm(ks[2], (D_MODEL, IN_COLS), D_MODEL ** -0.5),
        "q_norm_g": 1.0 + nrm(ks[3], (HEAD_DIM,), 0.02),
        "k_norm_g": 1.0 + nrm(ks[4], (HEAD_DIM,), 0.02),
        "attn_sink": nrm(ks[5], (N_HEADS,), 0.5),
        "hy_conv_w": nrm(ks[6], (SHORT_CONV, 3 * HYENA_WIDTH), SHORT_CONV ** -0.5),
        "hy_conv_b": nrm(ks[7], (3 * HYENA_WIDTH,), 0.02),
        "filt_w1": nrm(ks[8], (FILTER_EMB, FILTER_HIDDEN), FILTER_EMB ** -0.5),
        "filt_b1": nrm(ks[9], (FILTER_HIDDEN,), 0.02),
        "filt_freq1": 1.0 + nrm(ks[10], (FILTER_HIDDEN,), 0.02),
        "filt_w2": nrm(ks[11], (FILTER_HIDDEN, FILTER_HIDDEN), FILTER_HIDDEN ** -0.5),
        "filt_b2": nrm(ks[12], (FILTER_HIDDEN,), 0.02),
        "filt_freq2": 1.0 + nrm(ks[13], (FILTER_HIDDEN,), 0.02),
        "filt_w3": nrm(ks[14], (FILTER_HIDDEN, FILTER_OUT), FILTER_HIDDEN ** -0.5),
        "hy_bias_d": nrm(ks[15], (HYENA_ORDER, HYENA_WIDTH), 1.0),
        "w_attn_proj": nrm(ks[16], (ATTN_WIDTH, D_MODEL), ATTN_WIDTH ** -0.5),
        "w_hyena_proj": nrm(ks[17], (HYENA_WIDTH, D_MODEL), HYENA_WIDTH ** -0.5),
        "w_out": nrm(ks[18], (D_MODEL, D_MODEL), D_MODEL ** -0.5),
        "norm2_g": 1.0 + nrm(ks[19], (D_MODEL,), 0.02),
        "w_gate": nrm(ks[20], (D_MODEL, D_FF), D_MODEL ** -0.5),
        "w_up": nrm(ks[21], (D_MODEL, D_FF), D_MODEL ** -0.5),
        "w_down": nrm(ks[22], (D_FF, D_MODEL), D_FF ** -0.5),
    }


def reference(x, norm1_g, w_in, q_norm_g, k_norm_g, attn_sink, hy_conv_w, hy_conv_b,
              filt_w1, filt_b1, filt_freq1, filt_w2, filt_b2, filt_freq2, filt_w3, hy_bias_d,
              w_attn_proj, w_hyena_proj, w_out, norm2_g, w_gate, w_up, w_down):
    split_at = np.cumsum([ATTN_WIDTH, KV_WIDTH, KV_WIDTH, 3 * HYENA_WIDTH, D_MODEL]).tolist()
    for _ in range(DEPTH):
        h = rms_norm(x, norm1_g)
        proj = h @ w_in
        q, k, v, hy_in, gate_a, gate_h = jnp.split(proj, split_at, axis=-1)
        attn = windowed_attention(q, k, v, q_norm_g, k_norm_g, attn_sink) @ w_attn_proj
        hyena = hyena_mixer(hy_in, hy_conv_w, hy_conv_b, filt_w1, filt_b1, filt_freq1,
                            filt_w2, filt_b2, filt_freq2, filt_w3, hy_bias_d) @ w_hyena_proj
        mixed = jax.nn.sigmoid(gate_a) * attn + jax.nn.sigmoid(gate_h) * hyena
        x = x + mixed @ w_out
        h2 = rms_norm(x, norm2_g)
        x = x + (jax.nn.silu(h2 @ w_gate) * (h2 @ w_up)) @ w_down
    return x.astype(x.dtype)
```

```python
import contextlib
import math
import numpy as np
import concourse.bass as bass
import concourse.mybir as mybir
from concourse.bass_utils import run_bass_kernel_spmd

F32 = mybir.dt.float32
BF16 = mybir.dt.bfloat16
AF = mybir.ActivationFunctionType
ALU = mybir.AluOpType
AX = mybir.AxisListType

NFFT = 8192
L = 4096


class Prog:
    ENGS = ["sync", "scalar", "vector", "gpsimd", "tensor"]

    def __init__(self, nc, stack):
        self.nc = nc
        self.stack = stack
        self.ops = {e: [] for e in self.ENGS}
        self.tick = {e: 0 for e in self.ENGS}
        self.esem = {e: stack.enter_context(nc.semaphore("es_" + e)) for e in self.ENGS}
        self.waited = {e: {} for e in self.ENGS}
        self.lastw = {}
        self.readers = {}
        self.dsem = {}
        self.same_engine_sync = True

    def _need(self, e, ev, waits):
        if ev is None:
            return
        if ev[0] == "e":
            if ev[1] == e and (not self.same_engine_sync or e == "tensor"):
                return
            k = "e_" + ev[1]
            sem = self.esem[ev[1]]
        else:
            k = "d_" + ev[1]
            sem = self.dsem[ev[1]][0]
        if self.waited[e].get(k, 0) >= ev[2]:
            return
        cur = waits.get(k)
        if cur is None or cur[1] < ev[2]:
            waits[k] = (sem, ev[2])

    def _deps(self, e, reads, writes):
        waits = {}
        for k in reads:
            self._need(e, self.lastw.get(k), waits)
        for k in writes:
            self._need(e, self.lastw.get(k), waits)
            for ev in self.readers.get(k, []):
                self._need(e, ev, waits)
        for k, (sem, v) in waits.items():
            self.waited[e][k] = v
        return list(waits.values())

    def _commit(self, ev, reads, writes):
        for k in reads:
            lst = self.readers.setdefault(k, [])
            lst[:] = [x for x in lst if not (x[0] == ev[0] and x[1] == ev[1])]
            lst.append(ev)
        for k in writes:
            self.lastw[k] = ev
            self.readers[k] = []

    def op(self, e, fn, reads=(), writes=()):
        waits = self._deps(e, reads, writes)
        self.tick[e] += 1
        t = self.tick[e]
        sem = self.esem[e]
        eng = getattr(self.nc, e)

        def run():
            for (s, v) in waits:
                eng.wait_ge(s, v)
            fn(eng).then_inc(sem, 1)
        self.ops[e].append(run)
        self._commit(("e", e, t), reads, writes)

    def dma(self, q, slot, out, in_, reads=(), writes=(), **kw):
        if slot not in self.dsem:
            self.dsem[slot] = [self.stack.enter_context(self.nc.semaphore("ds_" + slot)), 0]
        waits = self._deps(q, reads, writes)
        self.dsem[slot][1] += 16
        cnt = self.dsem[slot][1]
        sem = self.dsem[slot][0]
        eng = getattr(self.nc, q)

        def run():
            for (s, v) in waits:
                eng.wait_ge(s, v)
            eng.dma_start(out=out, in_=in_, **kw).then_inc(sem, 16)
        self.ops[q].append(run)
        self._commit(("d", slot, cnt), reads, writes)

    def finish(self, e, keys):
        waits = self._deps(e, keys, [])
        eng = getattr(self.nc, e)

        def run():
            for (s, v) in waits:
                eng.wait_ge(s, v)
        self.ops[e].append(run)

    def emit(self):
        with self.nc.Block() as block:
            @block.sync
            def _(x):
                for f in self.ops["sync"]:
                    f()

            @block.scalar
            def _(x):
                for f in self.ops["scalar"]:
                    f()

            @block.vector
            def _(x):
                for f in self.ops["vector"]:
                    f()

            @block.gpsimd
            def _(x):
                for f in self.ops["gpsimd"]:
                    f()

            @block.tensor
            def _(x):
                for f in self.ops["tensor"]:
                    f()


def fft_tables():
    N = NFFT
    T = {}
    f1h = np.arange(64) + 0.5
    t1 = np.arange(128)[:, None]
    ang = 2 * np.pi * t1 * f1h[None, :] / 128.0
    F1 = np.zeros((128, 128))
    for h in range(2):
        F1[:, h * 64:h * 64 + 32] = np.cos(ang[:, 32 * h:32 * h + 32])
        F1[:, h * 64 + 32:h * 64 + 64] = -np.sin(ang[:, 32 * h:32 * h + 32])
    T["F1"] = F1
    t2 = np.arange(64)
    f2 = np.arange(64)
    G = np.zeros((128, 32, 2, 128))
    for h in range(2):
        for f1l in range(32):
            f1 = 32 * h + f1l
            th = 2 * np.pi * t2[:, None] * (f1h[f1] + 128 * f2[None, :]) / N
            c, s = np.cos(th), np.sin(th)
            G[h * 64:(h + 1) * 64, f1l, 0, 0:64] = c
            G[h * 64:(h + 1) * 64, f1l, 0, 64:128] = -s
            G[h * 64:(h + 1) * 64, f1l, 1, 0:64] = s
            G[h * 64:(h + 1) * 64, f1l, 1, 64:128] = c
    T["G"] = G.reshape(128, 32 * 2 * 128)
    phi = 2 * np.pi * f2[:, None] * np.arange(64)[None, :] / 64.0
    c, s = np.cos(phi), np.sin(phi)
    Fre = np.concatenate([c, -s], axis=0)
    Fim = np.concatenate([s, c], axis=0)

    def mk(Fo):
        Fr, Fi = Fo[0:64], Fo[64:128]
        return np.concatenate([Fr, -Fr], axis=0), np.concatenate([Fi, Fi], axis=0)
    R1re, R2re = mk(Fre)
    R1im, R2im = mk(Fim)
    T["R"] = np.stack([R1re, R2re, R1im, R2im], axis=1).reshape(128, 4 * 64)
    tau1 = np.arange(64)
    H = np.zeros((128, 64, 64))
    for tau2 in range(64):
        psi = 2 * np.pi * f1h[:, None] * (64 * tau1[None, :] + tau2) / N
        H[0:64, tau2, :] = 2.0 / N * np.cos(psi)
        H[64:128, tau2, :] = -2.0 / N * np.sin(psi)
    T["H"] = H.reshape(128, 64 * 64)
    Pm = np.zeros((128, 128))
    for p in range(128):
        Pm[p, (p + 64) % 128] = 1.0
    T["perm"] = Pm
    T["ident"] = np.eye(128)
    return {k: np.ascontiguousarray(v, dtype=np.float32) for k, v in T.items()}


class Ctx:
    pass


def alt(i):
    return "scalar" if (i % 2 == 0) else "vector"


def copy_op(P, e, out, in_, reads, writes):
    if e == "scalar":
        P.op("scalar", lambda g: g.activation(out=out, in_=in_, func=AF.Copy), reads, writes)
    else:
        P.op(e, lambda g: g.tensor_copy(out=out, in_=in_), reads, writes)


def nb(C):
    C.bank = (getattr(C, "bank", -1) + 1) % 8
    return C.bank


def st_T(P, C, src, srckey, pr0, nt1, perm_src=False):
    uT = C.uT
    if perm_src:
        srcv = src[:, 0:nt1 * 64].rearrange("p (b a) -> p b a", a=nt1)
    else:
        srcv = src[:, 0:nt1 * 64].rearrange("p (a b) -> p b a", b=64)
    for g in range(4):
        bank = nb(C)
        key = "ps%d" % bank
        pb = C.ps[bank][:].bitcast(BF16)
        for s in range(16):
            t2 = g * 16 + s
            P.op("tensor",
                 lambda e, t2=t2, s=s, pb=pb: e.transpose(
                     out=pb[0:nt1, s * 64:(s + 1) * 64], in_=srcv[pr0:pr0 + 64, t2, :],
                     identity=C.ident[pr0:pr0 + 64, pr0:pr0 + 64]),
                 reads=[srckey], writes=[key])
        copy_op(P, alt(g), uT[0:nt1, g * 1024:(g + 1) * 1024], pb[0:nt1, 0:1024], [key], ["uT"])


def st_S1(P, C, nt1, F1t):
    uTv = C.uT[:, :].rearrange("p (b c) -> p b c", c=64)
    Av = C.A[:, :].rearrange("p (k c) -> p k c", c=64)
    for g in range(8):
        bank = nb(C)
        key = "ps%d" % bank
        pa = C.ps[bank]
        pav = pa[:, :].rearrange("p (k s) -> p s k", s=8)
        for s in range(8):
            c = g * 8 + s
            for h in range(2):
                P.op("tensor",
                     lambda e, c=c, s=s, h=h, pav=pav: e.matmul(
                         pav[h * 64:(h + 1) * 64, s, :], uTv[0:nt1, :, c],
                         F1t[0:nt1, h * 64:(h + 1) * 64], start=True, stop=True),
                     reads=["uT"], writes=[key])
        copy_op(P, alt(g), Av[:, :, g * 8:(g + 1) * 8], pa[:, :].rearrange("p (k s) -> p k s", s=8), [key], ["A"])


def st_S2(P, C, evac_kind, Kp, Kpp, Q1=None, Q2=None):
    Av = C.A[:, :].rearrange("p (k c) -> p k c", c=64)
    Gv = C.G[:, :].rearrange("p (f r m) -> p f r m", r=2, m=128)
    for g in range(8):
        bank = nb(C)
        key = "ps%d" % bank
        px = C.ps[bank]
        for s in range(8):
            f1 = g * 8 + s
            h, f1l = f1 // 32, f1 % 32
            for r in range(2):
                P.op("tensor",
                     lambda e, s=s, h=h, f1l=f1l, r=r, px=px: e.matmul(
                         px[:, s * 64:(s + 1) * 64], Gv[h * 64:(h + 1) * 64, f1l, r, :],
                         Av[h * 64:(h + 1) * 64, 32 * r + f1l, :], start=(r == 0), stop=(r == 1)),
                     reads=["A"], writes=[key])
        sl = slice(g * 512, (g + 1) * 512)
        if evac_kind == "kernel":
            copy_op(P, alt(g), Kp[:, sl], px[:, :], [key], ["Kp"])
        else:
            P.op("vector", lambda e, sl=sl, px=px: e.tensor_tensor(out=Q1[:, sl], in0=px[:, :], in1=Kp[:, sl], op=ALU.mult),
                 reads=[key, "Kp"], writes=["Q1"])
            P.op("vector", lambda e, sl=sl, px=px: e.tensor_tensor(out=Q2[:, sl], in0=px[:, :], in1=Kpp[:, sl], op=ALU.mult),
                 reads=[key, "Kpp"], writes=["Q2"])


def st_perm(P, C, Kp, Kpp):
    for g in range(8):
        bank = nb(C)
        key = "ps%d" % bank
        sl = slice(g * 512, (g + 1) * 512)
        P.op("tensor", lambda e, sl=sl, bank=bank: e.matmul(C.ps[bank][:, :], C.perm[:, :], Kp[:, sl], start=True, stop=True),
             reads=["Kp"], writes=[key])
        copy_op(P, alt(g + 1), Kpp[:, sl], C.ps[bank][:, :], [key], ["Kpp"])


def st_I1(P, C, Q1, Q2):
    Q1v = Q1[:, :].rearrange("p (f c) -> p f c", c=64)
    Q2v = Q2[:, :].rearrange("p (f c) -> p f c", c=64)
    Rv = C.R[:, :].rearrange("p (r t) -> p r t", t=64)
    for g in range(8):
        bank = nb(C)
        key = "ps%d" % bank
        pb = C.ps[bank]
        for s in range(8):
            c = g * 8 + s
            for ri in range(2):
                P.op("tensor",
                     lambda e, c=c, s=s, ri=ri, pb=pb: e.matmul(
                         pb[ri * 64:(ri + 1) * 64, s * 64:(s + 1) * 64], Q1v[:, :, c], Rv[:, 2 * ri, :],
                         start=True, stop=False),
                     reads=["Q1"], writes=[key])
                P.op("tensor",
                     lambda e, c=c, s=s, ri=ri, pb=pb: e.matmul(
                         pb[ri * 64:(ri + 1) * 64, s * 64:(s + 1) * 64], Q2v[:, :, c], Rv[:, 2 * ri + 1, :],
                         start=False, stop=True),
                     reads=["Q2"], writes=[key])
        copy_op(P, alt(g), C.Bs[:, g * 512:(g + 1) * 512], pb[:, :], [key], ["Bs"])


def st_I2(P, C, pr0, epilogue):
    Bv = C.Bs[:, :].rearrange("p (c t) -> p c t", t=64)
    Hv = C.H[:, :].rearrange("p (t a) -> p t a", a=64)
    for g in range(8):
        bank = nb(C)
        key = "ps%d" % bank
        py = C.ps[bank]
        for s in range(8):
            tau2 = g * 8 + s
            P.op("tensor",
                 lambda e, tau2=tau2, s=s, py=py: e.matmul(
                     py[pr0:pr0 + 64, s * 64:(s + 1) * 64], Bv[:, :, tau2], Hv[:, tau2, :],
                     start=True, stop=True),
                 reads=["Bs"], writes=[key])
        epilogue(P, g, py[pr0:pr0 + 64, :].rearrange("p (s a) -> p s a", a=64), key)


def fft_fwd(P, C, src, srckey, pr0, nt1, evac_kind, Kp=None, Kpp=None, Q1=None, Q2=None, F1t=None):
    if F1t is None:
        F1t = C.F1
    st_T(P, C, src, srckey, pr0, nt1)
    st_S1(P, C, nt1, F1t)
    st_S2(P, C, evac_kind, Kp, Kpp, Q1, Q2)
    if evac_kind == "kernel":
        st_perm(P, C, Kp, Kpp)


def fft_inv(P, C, Q1, Q2, pr0, epilogue):
    st_I1(P, C, Q1, Q2)
    st_I2(P, C, pr0, epilogue)


def tview(buf, pr0, g):
    return buf[pr0:pr0 + 64, g * 512:(g + 1) * 512].rearrange("p (s a) -> p s a", a=64)


D = 1024
SEQ = 4096
NH = 8
HD = 64
DFF = 2816
NFC = DFF // 128
HW = 512
OWN = 2048
EXT = 2304
NROW = EXT + OWN + 128
EPS = 1e-6
NEG = -1e30


def barrier(P):
    for e in P.ENGS:
        waits = []
        for e2 in P.ENGS:
            if P.tick[e2] > P.waited[e].get("e_" + e2, 0) and (e2 != e or e != "tensor"):
                waits.append((P.esem[e2], P.tick[e2]))
                P.waited[e]["e_" + e2] = P.tick[e2]
        for slot, (sem, cnt) in P.dsem.items():
            if slot in getattr(P, "nobarrier", ()):
                continue
            if cnt > P.waited[e].get("d_" + slot, 0):
                waits.append((sem, cnt))
                P.waited[e]["d_" + slot] = cnt
        eng = getattr(P.nc, e)

        def run(waits=waits, eng=eng):
            for (s, v) in waits:
                eng.wait_ge(s, v)
        P.ops[e].append(run)
    keep = {k: v for k, v in P.lastw.items() if v[0] == "d" and v[1] in getattr(P, "nobarrier", ())}
    P.lastw.clear()
    P.lastw.update(keep)
    P.readers.clear()


def mm(P, out, lhsT, rhs, start, stop, reads, writes):
    P.op("tensor", lambda e: e.matmul(out, lhsT, rhs, start=start, stop=stop), reads, writes)


def host_tables(par):
    T = fft_tables()
    N = NFFT
    f1h = np.arange(64) + 0.5
    t1 = np.arange(64).astype(np.float64)
    t1e = np.where((t1 >= 32) & (par == 1), t1 - 64, t1)[:, None]
    ang = 2 * np.pi * t1e * f1h[None, :] / 128.0
    F1d = np.zeros((128, 128))
    for h in range(2):
        F1d[0:64, h * 64:h * 64 + 32] = np.cos(ang[:, 32 * h:32 * h + 32])
        F1d[0:64, h * 64 + 32:h * 64 + 64] = -np.sin(ang[:, 32 * h:32 * h + 32])
    T["F1d"] = F1d.astype(np.float32)
    tau1 = np.arange(64)
    H = np.zeros((128, 64, 64))
    for tau2 in range(64):
        n = 64.0 * tau1 + tau2
        n = np.where((tau1 >= 32) & (par == 1), n - 4096, n)
        psi = 2 * np.pi * f1h[:, None] * n[None, :] / N
        H[0:64, tau2, :] = 2.0 / N * np.cos(psi)
        H[64:128, tau2, :] = -2.0 / N * np.sin(psi)
    T["H"] = H.reshape(128, 4096).astype(np.float32)
    slot = np.arange(NFFT)
    lag = np.where(slot <= L, slot, NFFT - slot)
    lag = np.minimum(lag, L - 1)
    tt = np.linspace(0.0, 1.0, L, dtype=np.float32).astype(np.float64)[lag]
    w = (2.0 * np.pi * np.arange(L, dtype=np.float32) / L).astype(np.float64)[lag]
    bands = np.linspace(1e-4, 15, 16, dtype=np.float32).astype(np.float64)
    zT = np.concatenate([tt[None, :], np.cos(bands[:, None] * w[None, :]), -np.sin(bands[:, None] * w[None, :])], axis=0)
    T["zT"] = zT.astype(np.float32)
    max_decay = math.log(1e-2) / 0.3
    min_decay = math.log(1e-2) / 1.5
    deltas = np.abs(np.linspace(min_decay, max_decay, HW, dtype=np.float32).astype(np.float64))
    sc = deltas / (L - 1)
    dsc = np.stack([-sc, sc], axis=1)
    dbi = np.zeros((HW, 16))
    for k in range(8):
        dbi[:, k] = -sc * 512 * k
        dbi[:, 8 + k] = -sc * (4096 - 512 * k)
    T["dsc"] = dsc.reshape(4, 128, 2).transpose(1, 0, 2).reshape(128, 8).astype(np.float32)
    T["dbi"] = dbi.reshape(4, 128, 16).transpose(1, 0, 2).reshape(128, 64).astype(np.float32)
    T["iota"] = np.tile(np.arange(512, dtype=np.float32)[None, :], (128, 1))
    slopes = 2.0 ** (-8.0 * np.arange(1, 9) / 8.0)
    s = np.arange(128)[:, None]
    q = np.arange(128)[None, :]
    bt = np.zeros((128, 5, 8, 128))
    for kind in range(5):
        rel = [-1, 0, 1, -1, 1][kind]
        dist = np.abs(q - (s + 128 * rel))
        for h in range(8):
            b = np.where(dist <= 128, -slopes[h] * dist, NEG)
            if (kind == 3 and par == 0) or (kind == 4 and par == 1):
                b = np.full_like(b, NEG)
            bt[:, kind, h, :] = b
    T["biasT"] = bt.reshape(128, 5 * 8 * 128).astype(np.float32)
    T["ones"] = np.ones((128, 128), np.float32)
    import ml_dtypes
    T["biasT"] = T["biasT"].astype(ml_dtypes.bfloat16)
    return T


def host_inputs(inp, core):
    b, par = core // 2, core % 2
    f = lambda a: np.ascontiguousarray(np.asarray(a), dtype=np.float32)
    x = f(inp["x"])[b]
    t0 = OWN * par
    o0 = OWN * (1 - par)
    xin = np.zeros((NROW, D), np.float32)
    xin[0:OWN] = x[t0:t0 + OWN]
    xin[OWN:2 * OWN] = x[o0:o0 + OWN]
    if t0 - 128 >= 0:
        xin[2 * OWN:2 * OWN + 128] = x[t0 - 128:t0]
    if t0 + OWN + 128 <= SEQ:
        xin[2 * OWN + 128:2 * OWN + 256] = x[t0 + OWN:t0 + OWN + 128]
    hb = 2 * OWN + 256
    for i, t in enumerate([t0 - 1, t0 + OWN, o0 - 1, o0 + OWN]):
        if 0 <= t < SEQ:
            xin[hb + i] = x[t]
    d = {"x_in": xin}
    w_in = f(inp["w_in"])
    d["w_qkv"] = np.ascontiguousarray(w_in[:, 0:768])
    why = w_in[:, 768:2304].reshape(D, 3, 4, 128).transpose(0, 2, 1, 3)
    d["w_hy"] = np.ascontiguousarray(why.reshape(D, 1536))
    d["w_g"] = np.ascontiguousarray(w_in[:, 2304:4352])
    d["g1"] = np.ascontiguousarray(f(inp["norm1_g"]).reshape(8, 128).T)
    d["g2"] = np.ascontiguousarray(f(inp["norm2_g"]).reshape(8, 128).T)
    d["gq"] = np.ascontiguousarray(np.tile(f(inp["q_norm_g"]), 2)[:, None])
    d["gk"] = np.ascontiguousarray(np.tile(f(inp["k_norm_g"]), 2)[:, None])
    d["sink"] = f(inp["attn_sink"])[None, :]
    cw = f(inp["hy_conv_w"]).reshape(3, 3, 4, 128)
    d["cw"] = np.ascontiguousarray(cw.transpose(3, 2, 1, 0).reshape(128, 36))
    cb = f(inp["hy_conv_b"]).reshape(3, 4, 128)
    d["cb"] = np.ascontiguousarray(cb.transpose(2, 1, 0).reshape(128, 12))
    d["fw1"] = f(inp["filt_w1"])
    d["fb1"] = f(inp["filt_b1"])[:, None]
    d["ff1"] = f(inp["filt_freq1"])[:, None]
    d["fw2"] = f(inp["filt_w2"])
    d["fb2"] = f(inp["filt_b2"])[:, None]
    d["ff2"] = f(inp["filt_freq2"])[:, None]
    d["fw3"] = f(inp["filt_w3"])
    dd = f(inp["hy_bias_d"]).reshape(2, 4, 128)
    d["dd"] = np.ascontiguousarray(dd.transpose(2, 1, 0).reshape(128, 8))
    d["w_ap"] = f(inp["w_attn_proj"])
    d["w_hp"] = f(inp["w_hyena_proj"])
    d["w_out"] = f(inp["w_out"])
    d["w_gate"] = f(inp["w_gate"])
    d["w_up"] = f(inp["w_up"])
    d["w_down"] = f(inp["w_down"])
    for k, v in host_tables(par).items():
        d["t_" + k] = v
    return d


IN_SHAPES = None
MAGIC = 12582912.0
PI_SAFE = 3.141592
TWO_PI = 2.0 * math.pi


def flush(P):
    barrier(P)
    P.emit()
    for e in P.ENGS:
        P.ops[e] = []


def wload(P, dst, src, nch, ncols, key, q="gpsimd", col0=0, piece=2048):
    dv = dst.rearrange("p (ch n) -> p ch n", ch=nch)
    sv = src.rearrange("(ch p) n -> p ch n", p=128)
    for c0 in range(0, ncols, piece):
        w = min(piece, ncols - c0)
        P.dma(q, key, dv[:, :, c0:c0 + w], sv[:, :, col0 + c0:col0 + c0 + w], writes=[key])


def rms_rstd(P, ss_ap, ms_ap, rstd_ap, neghalf, n, k):
    P.op("vector", lambda e: e.tensor_scalar(out=ms_ap, in0=ss_ap, scalar1=1.0 / n, scalar2=EPS, op0=ALU.mult, op1=ALU.add),
         reads=["ss" + k], writes=["ms" + k])
    P.op("gpsimd", lambda e: e.tensor_tensor(out=rstd_ap, in0=ms_ap, in1=neghalf, op=ALU.pow),
         reads=["ms" + k], writes=["rstd" + k])


def norm_stats(P, C, xb, xkey, junk, ss, ms, rstd, col, k, junkkey="junk"):
    P.op("scalar", lambda e: e.activation(out=junk[:, :], in_=xb[:, :], func=AF.Square, accum_out=ss[:, col:col + 1]),
         reads=[xkey], writes=[junkkey, "ss" + k])
    rms_rstd(P, ss[:, col:col + 1], ms[:, col:col + 1], rstd[:, col:col + 1], C.neghalf[:, 0:1], float(D), k)


def norm_finish(P, C, xb, xkey, xn, xnkey, rstd, col, g8, dst_view, dstkey, banks):
    k = xnkey
    P.op("scalar", lambda e: e.activation(out=xn[:, :], in_=xb[:, :], func=AF.Copy, scale=rstd[:, col:col + 1]),
         reads=[xkey, "rstd" + k], writes=[xnkey])
    for half in range(2):
        bank = banks[half]
        key = "ps%d" % bank
        for j in range(4):
            ch = half * 4 + j
            P.op("tensor", lambda e, ch=ch, j=j, bank=bank: e.transpose(
                out=C.ps[bank][:, j * 128:(j + 1) * 128], in_=xn[:, ch * 128:(ch + 1) * 128], identity=C.identf[:, :]),
                reads=[xnkey], writes=[key])
        P.op("vector", lambda e, half=half, bank=bank: e.tensor_tensor(
            out=dst_view[:, half * 4:(half + 1) * 4, :], in0=C.ps[bank][:, :].rearrange("p (j t) -> p j t", t=128),
            in1=g8[:, half * 4:(half + 1) * 4].unsqueeze(2).broadcast_to([128, 4, 128]), op=ALU.mult),
            reads=[key], writes=[dstkey])


BF16_INPUTS = ("t_biasT",)


def build_program(shapes, dbg=None):
    nc = bass.Bass("TRN2", target_bir_lowering=False)
    din = {}
    for k, shp in shapes.items():
        din[k] = nc.dram_tensor(k, list(shp), BF16 if k in BF16_INPUTS else F32, kind="ExternalInput").ap()
    out_d = nc.dram_tensor("out", [OWN, D], F32, kind="ExternalOutput").ap()
    hT_d = nc.dram_tensor("hT_d", [9 * 128, 8 * 512], BF16, kind="Internal").ap()
    hT2_d = nc.dram_tensor("hT2_d", [D, OWN], BF16, kind="Internal").ap()
    x1_d = nc.dram_tensor("x1_d", [OWN, D], F32, kind="Internal").ap()
    hTg = lambda g: hT_d[g * 128:(g + 1) * 128, :].rearrange("p (ch n) -> p ch n", ch=8)
    hT2v = hT2_d.rearrange("(ch p) n -> p ch n", p=128)
    dbg_d = {}
    if dbg:
        for k, shp in dbg.items():
            dbg_d[k] = nc.dram_tensor("dbg_" + k, list(shp), F32, kind="ExternalOutput").ap()

    with contextlib.ExitStack() as st:
        P = Prog(nc, st)
        C = Ctx()
        C.ps = [st.enter_context(nc.psum_tensor("ps%d" % i, [128, 512], F32)) for i in range(8)]
        sb = lambda stack, name, shape, dt: stack.enter_context(nc.sbuf_tensor("s_" + name, shape, dt))
        C.identf = sb(st, "identf", [128, 128], F32)
        C.ident = sb(st, "identb", [128, 128], BF16)
        C.onesb = sb(st, "onesb", [128, 128], BF16)
        C.onesf = sb(st, "onesf", [128, 128], F32)
        C.neghalf = sb(st, "neghalf", [128, 1], F32)
        g1 = sb(st, "g1", [128, 8], F32)
        g2 = sb(st, "g2", [128, 8], F32)
        mid = contextlib.ExitStack()
        hyT = sb(mid, "hyT", [128, 4 * OWN], BF16)
        h2T = sb(mid, "h2T", [128, 4096], BF16)
        wqkv = sb(mid, "wqkv", [128, 8 * 768], BF16)
        hyTv = hyT[:, :].rearrange("p (k n) -> p k n", k=4)
        P.dma("sync", "identf", C.identf[:, :], din["t_ident"][:, :], writes=["identf"])
        P.dma("sync", "onesf", C.onesf[:, :], din["t_ones"][:, :], writes=["onesf"])
        P.dma("gpsimd", "identb", C.ident[:, :], din["t_ident"][:, :], writes=["identb"])
        P.dma("gpsimd", "onesb", C.onesb[:, :], din["t_ones"][:, :], writes=["onesb"])
        P.dma("sync", "g1", g1[:, :], din["g1"][:, :], writes=["g1"])
        P.dma("sync", "g2", g2[:, :], din["g2"][:, :], writes=["g2"])
        P.op("vector", lambda e: e.memset(C.neghalf[:, :], -0.5), writes=["neghalf"])
        barrier(P)

        hy = contextlib.ExitStack()
        C.F1 = sb(hy, "F1", [128, 128], BF16)
        C.F1d = sb(hy, "F1d", [128, 128], BF16)
        C.G = sb(hy, "G", [128, 8192], BF16)
        C.R = sb(hy, "R", [128, 256], BF16)
        C.H = sb(hy, "H", [128, 4096], BF16)
        C.perm = sb(hy, "perm", [128, 128], BF16)
        P.dma("gpsimd", "F1", C.F1[:, :], din["t_F1"][:, :], writes=["F1"])
        P.dma("gpsimd", "F1d", C.F1d[:, :], din["t_F1d"][:, :], writes=["F1d"])
        P.dma("gpsimd", "R", C.R[:, :], din["t_R"][:, :], writes=["R"])
        P.dma("gpsimd", "perm", C.perm[:, :], din["t_perm"][:, :], writes=["perm"])
        for i in range(4):
            P.dma("gpsimd", "G", C.G[:, i * 2048:(i + 1) * 2048], din["t_G"][:, i * 2048:(i + 1) * 2048], writes=["G"])
        for i in range(2):
            P.dma("gpsimd", "H", C.H[:, i * 2048:(i + 1) * 2048], din["t_H"][:, i * 2048:(i + 1) * 2048], writes=["H"])
        w3b = sb(hy, "w3b", [128, 2048], BF16)
        for hf in range(2):
            P.dma("gpsimd", "w3b", w3b[hf * 64:(hf + 1) * 64, :], din["fw3"][:, :], writes=["w3b"])
        iota = sb(hy, "iota", [128, 512], F32)
        P.dma("sync", "iota", iota[:, :], din["t_iota"][:, :], writes=["iota"])
        cw = sb(hy, "cw", [128, 36], F32)
        cb = sb(hy, "cb", [128, 12], F32)
        ddt = sb(hy, "ddt", [128, 8], F32)
        dsc = sb(hy, "dsc", [128, 8], F32)
        dbi = sb(hy, "dbi", [128, 64], F32)
        for nm, t in [("cw", cw), ("cb", cb), ("dd", ddt), ("t_dsc", dsc), ("t_dbi", dbi)]:
            P.dma("sync", "c_" + nm, t[:, :], din[nm][:, :], writes=[nm])

        with contextlib.ExitStack() as ph:
            xt = [sb(ph, "p1_xt%d" % i, [128, D], F32) for i in range(6)]
            xn = [sb(ph, "p1_xn%d" % i, [128, D], F32) for i in range(2)]
            junk = sb(ph, "p1_junk", [128, D], BF16)
            hs = [sb(ph, "p1_hs%d" % i, [128, 8 * 512], BF16) for i in range(2)]
            ss = sb(ph, "p1_ss", [128, 40], F32)
            ms = sb(ph, "p1_ms", [128, 40], F32)
            rstd = sb(ph, "p1_rstd", [128, 40], F32)
            zT = sb(ph, "zT", [33, NFFT], F32)
            h1T = sb(ph, "h1T", [128, 4096], F32)
            fw1 = sb(ph, "fw1", [33, 64], F32)
            fw2 = sb(ph, "fw2", [128, 64], F32)
            fv = sb(ph, "fv", [128, 8], F32)
            farg = sb(ph, "farg", [128, 2048], F32)
            ftq = sb(ph, "ftq", [128, 2048], F32)
            for i in range(4):
                P.dma("scalar", "zT", zT[:, i * 2048:(i + 1) * 2048], din["t_zT"][:, i * 2048:(i + 1) * 2048], writes=["zT"])
            P.dma("scalar", "fw1", fw1[:, :], din["fw1"][:, :], writes=["fw1"])
            for hf in range(2):
                P.dma("scalar", "fw2", fw2[hf * 64:(hf + 1) * 64, :], din["fw2"][:, :], writes=["fw2"])
                for j, nm in enumerate(["fb1", "ff1", "fb2", "ff2"]):
                    P.dma("scalar", "fv", fv[hf * 64:(hf + 1) * 64, j:j + 1], din[nm][:, :], writes=["fv"])
            for layer in range(2):
                fc, bc = 1 + 2 * layer, 2 * layer
                P.op("vector", lambda e, layer=layer, fc=fc: e.tensor_scalar(out=fv[:, 4 + 2 * layer:5 + 2 * layer], in0=fv[:, fc:fc + 1], scalar1=1.0 / TWO_PI, scalar2=None, op0=ALU.mult),
                     reads=["fv"], writes=["fv2"])
                P.op("vector", lambda e, layer=layer, bc=bc: e.tensor_tensor(out=fv[:, 5 + 2 * layer:6 + 2 * layer], in0=fv[:, 4 + 2 * layer:5 + 2 * layer], in1=fv[:, bc:bc + 1], op=ALU.mult),
                     reads=["fv", "fv2"], writes=["fv2"])

            def filt_step(layer, sc):
                dst = h1T if layer == 0 else h2T
                dstk = "h1T" if layer == 0 else "h2T"
                for j in range(4):
                    bank = 4 + j
                    key = "ps%d" % bank
                    col = sc * 2048 + j * 512
                    for hf in range(2):
                        if layer == 0:
                            mm(P, C.ps[bank][hf * 64:(hf + 1) * 64, :], fw1[0:33, :], zT[0:33, hf * 4096 + col:hf * 4096 + col + 512], True, True, ["zT", "fw1"], [key])
                        else:
                            mm(P, C.ps[bank][hf * 64:(hf + 1) * 64, :], fw2[hf * 64:(hf + 1) * 64, :], h1T[hf * 64:(hf + 1) * 64, col:col + 512], True, True, ["h1T", "fw2"], [key])
                    P.op("scalar", lambda e, bank=bank, j=j: e.activation(
                        out=farg[:, j * 512:(j + 1) * 512], in_=C.ps[bank][:, :], func=AF.Identity,
                        scale=fv[:, 4 + 2 * layer:5 + 2 * layer], bias=fv[:, 5 + 2 * layer:6 + 2 * layer]),
                        reads=[key, "fv", "fv2"], writes=["farg"])
                P.op("vector", lambda e: e.tensor_scalar(out=ftq[:, :], in0=farg[:, :], scalar1=MAGIC, scalar2=None, op0=ALU.add),
                     reads=["farg"], writes=["ftq"])
                P.op("vector", lambda e: e.scalar_tensor_tensor(out=ftq[:, :], in0=ftq[:, :], scalar=MAGIC, in1=farg[:, :], op0=ALU.subtract, op1=ALU.subtract),
                     reads=["ftq", "farg"], writes=["ftq"])
                P.op("scalar", lambda e: e.activation(out=dst[:, sc * 2048:(sc + 1) * 2048], in_=ftq[:, :], func=AF.Sin, scale=-6.2831845),
                     reads=["ftq"], writes=[dstk])

            filt_sched = {5: (0, 0), 11: (0, 1), 17: (1, 0), 23: (1, 1)}

            ntile = NROW // 128

            def p1_load(i):
                P.dma("sync", "xt%d" % (i % 6), xt[i % 6][:, :], din["x_in"][i * 128:(i + 1) * 128, :], writes=["xt%d" % (i % 6)])

            def p1_stats(i):
                norm_stats(P, C, xt[i % 6], "xt%d" % (i % 6), junk, ss, ms, rstd, i, "p1_%d" % (i % 2))

            for i in range(5):
                p1_load(i)
            p1_stats(0)
            for i in range(ntile):
                if i + 5 < ntile:
                    p1_load(i + 5)
                if i + 1 < ntile:
                    p1_stats(i + 1)
                grp, gi = i // 4, i % 4
                hsb = hs[grp % 2]
                hkey = "hs%d" % (grp % 2)
                dv = hsb[:, :].rearrange("p (ch n) -> p ch n", ch=8)[:, :, gi * 128:(gi + 1) * 128]
                norm_finish(P, C, xt[i % 6], "xt%d" % (i % 6), xn[i % 2], "p1_%d" % (i % 2), rstd, i, g1, dv, hkey,
                            [(i % 2) * 2, (i % 2) * 2 + 1])
                if i in filt_sched:
                    filt_step(*filt_sched[i])
                if gi == 3 or i == ntile - 1:
                    ncol = (gi + 1) * 128
                    P.dma("sync", "hst%d" % (grp % 2), hT_d[grp * 128:(grp + 1) * 128, :], hsb[:, :], reads=[hkey], writes=["hTd"])
            flush(P)
        if dbg and "hT" in dbg:
            with contextlib.ExitStack() as ph:
                t = sb(ph, "dbg_t", [128, 8 * 512], BF16)
                t2 = sb(ph, "dbg_t2", [128, 8 * 512], F32)
                for c0 in range(0, NROW, 512):
                    w = min(512, NROW - c0)
                    P.dma("sync", "dbgl", t[:, :].rearrange("p (ch n) -> p ch n", ch=8)[:, :, 0:w], hTv[:, :, c0:c0 + w], writes=["dt"])
                    P.op("vector", lambda e: e.tensor_copy(out=t2[:, :], in_=t[:, :]), reads=["dt"], writes=["dt2"])
                    P.dma("sync", "dbgs", dbg_d["hT"].rearrange("(ch p) n -> p ch n", p=128)[:, :, c0:c0 + w],
                          t2[:, :].rearrange("p (ch n) -> p ch n", ch=8)[:, :, 0:w], reads=["dt2"], writes=["dbgo"])
                flush(P)

        P.nobarrier = {"whb", "wqkv", "F1", "F1d", "R", "perm", "G", "H", "w3b", "iota", "c_cw", "c_cb", "c_dd", "c_t_dsc", "c_t_dbi"}
        if True:
            ph = hy
            C.uT = sb(ph, "uT", [128, 4096], BF16)
            C.A = sb(ph, "A", [128, 4096], BF16)
            C.Bs = sb(ph, "Bs", [128, 4096], BF16)
            U = sb(ph, "U", [128, 28800], BF16)
            vb = sb(ph, "vb", [128, 4096], BF16)
            x1b = sb(ph, "x1b", [128, 4096], BF16)
            x2b = sb(ph, "x2b", [128, 4096], BF16)
            zb = sb(ph, "zb", [128, 4096], BF16)
            dec_ = [sb(ph, "dec%d" % i, [128, 512], F32) for i in range(2)]
            whb = sb(ph, "whb", [128, 8 * 384], BF16)
            dg = sb(ph, "dg", [128, 9 * 128], BF16)
            fz = sb(ph, "fz", [128, 2], F32)
            stg_ = [sb(ph, "stg%d" % i, [128, 512], BF16) for i in range(2)]
            dgv = dg[:, :].rearrange("p (j n) -> p j n", j=9)
            hh = sb(ph, "hh", [128, 8 * 4], BF16)
            nrm = sb(ph, "nrm", [128, 4], F32)
            pre = U[:, 16384:16384 + 12300].rearrange("p (s g n) -> p s g n", s=3, g=2)
            hTt = [C.uT[:, 0:4096], C.A[:, 0:4096], C.Bs[:, 0:4096]]
            kern = U[:, 0:8192]
            Kp = U[:, 8192:12288]
            Kpp = U[:, 12288:16384]
            Q1 = U[:, 16384:20480]
            Q2 = U[:, 20480:24576]
            whv = whb[:, :].rearrange("p (ch n) -> p ch n", ch=8)
            hhv = hh[:, :].rearrange("p (ch n) -> p ch n", ch=8)
            for b in range(4):
                def kgen(o, b=b):
                    for ck in range(16):
                        dirn, kk = ck // 8, ck % 8
                        bank = nb(C)
                        key = "ps%d" % bank
                        dec, dkey = dec_[ck % 2], "dec%d" % (ck % 2)
                        col0 = ((dirn * 2 + o) * 512) + b * 128
                        mm(P, C.ps[bank][:, :], w3b[dirn * 64:(dirn + 1) * 64, col0:col0 + 128], h2T[dirn * 64:(dirn + 1) * 64, kk * 512:(kk + 1) * 512], True, True, ["w3b", "h2T"], [key])
                        P.op("scalar", lambda e, dirn=dirn, ck=ck, dec=dec: e.activation(
                            out=dec[:, :], in_=iota[:, :], func=AF.Exp, scale=dsc[:, b * 2 + dirn:b * 2 + dirn + 1], bias=dbi[:, b * 16 + ck:b * 16 + ck + 1]),
                            reads=["iota", "t_dsc", "t_dbi"], writes=[dkey])
                        P.op("vector", lambda e, ck=ck, bank=bank, dirn=dirn, dec=dec: e.scalar_tensor_tensor(
                            out=kern[:, ck * 512:(ck + 1) * 512], in0=C.ps[bank][:, :], scalar=(1.0 if dirn == 0 else -1.0), in1=dec[:, :], op0=ALU.mult, op1=ALU.mult),
                            reads=[key, dkey], writes=["kern"])
                    nk, rk = "nrm%d" % o, "rn%d" % o
                    n0, n1 = nrm[:, 2 * o:2 * o + 1], nrm[:, 2 * o + 1:2 * o + 2]
                    P.op("vector", lambda e: e.memset(kern[:, 4096:4097], 0.0), writes=["kern"])
                    P.op("vector", lambda e: e.tensor_reduce(out=n0, in_=kern[:, :], axis=AX.X, op=ALU.add, apply_absolute_value=True),
                         reads=["kern"], writes=[nk])
                    P.op("vector", lambda e: e.tensor_scalar(out=n0, in0=n0, scalar1=EPS, scalar2=None, op0=ALU.add), reads=[nk], writes=[nk])
                    P.op("vector", lambda e: e.reciprocal(out=n1, in_=n0), reads=[nk], writes=[rk])
                    P.op("vector", lambda e: e.scalar_tensor_tensor(
                        out=kern[:, 0:1], in0=n0, scalar=ddt[:, b * 2 + o:b * 2 + o + 1], in1=kern[:, 0:1], op0=ALU.mult, op1=ALU.add),
                        reads=[nk, "kern", "dd"], writes=["kern"])

                def Kst(o, hb):
                    pr0 = hb * 64
                    return [lambda: st_T(P, C, kern, "kern", pr0, 128),
                            lambda: st_S1(P, C, 128, C.F1),
                            lambda: st_S2(P, C, "kernel", Kp, Kpp),
                            lambda: st_perm(P, C, Kp, Kpp)]

                def Dst(o, hb, b=b):
                    pr0 = hb * 64
                    src, srck = (vb, "sig0") if o == 0 else (zb, "zb")
                    rcol = nrm[pr0:pr0 + 64, 2 * o + 1:2 * o + 2]
                    if o == 0:
                        def epi(P, g, psv, key):
                            P.op("vector", lambda e: e.scalar_tensor_tensor(
                                out=tview(zb, pr0, g), in0=psv, scalar=rcol, in1=tview(x1b, pr0, g), op0=ALU.mult, op1=ALU.mult),
                                reads=[key, "rn0", "sig1"], writes=["zb"])
                    else:
                        def epi(P, g, psv, key):
                            ov = hyT[pr0:pr0 + 64, b * OWN:(b + 1) * OWN].rearrange("p (a t) -> p t a", t=64)[:, g * 8:(g + 1) * 8, :]
                            P.op("vector", lambda e: e.scalar_tensor_tensor(
                                out=ov, in0=psv[:, :, 0:32], scalar=rcol, in1=tview(x2b, pr0, g)[:, :, 0:32], op0=ALU.mult, op1=ALU.mult),
                                reads=[key, "rn1", "sig2"], writes=["hyT"])
                    return [lambda: st_T(P, C, src, srck, pr0, 64, perm_src=True),
                            lambda: st_S1(P, C, 64, C.F1d),
                            lambda: st_S2(P, C, "data", Kp, Kpp, Q1, Q2),
                            lambda: st_I1(P, C, Q1, Q2),
                            lambda: st_I2(P, C, pr0, epi)]

                if b == 0:
                    kgen(0)
                    for f in Kst(0, 0):
                        f()
                if b == 0:
                    wload(P, whb[:, :], din["w_hy"], 8, 384, "whb", col0=0)
                P.dma("sync", "hh", hhv, hTg(8)[:, :, 256:260], writes=["hh"])
                for sig in range(3):
                    bank = nb(C)
                    key = "ps%d" % bank
                    for ch in range(8):
                        mm(P, C.ps[bank][:, 0:4], whv[:, ch, sig * 128:(sig + 1) * 128], hhv[:, ch, :], ch == 0, ch == 7, ["whb", "hh"], [key])
                    for j, (seg, pos) in enumerate([(0, 0), (0, 2049), (1, 0), (1, 2049)]):
                        copy_op(P, alt(j), pre[:, sig, seg, pos:pos + 1], C.ps[bank][:, j:j + 1], [key], ["pre%d" % seg, "Q1", "Q2"])

                for sig in range(3):
                    for j in range(3):
                        wi = (b * 3 + sig) * 3 + j
                        P.op("vector", lambda e, sig=sig, j=j, wi=wi: e.tensor_scalar(out=dgv[:, sig * 3 + j, :], in0=C.ident[:, :], scalar1=cw[:, wi:wi + 1], scalar2=None, op0=ALU.mult),
                             reads=["identb", "cw"], writes=["dg"])

                def shortconv_steps(seg, b=b):
                    steps = []
                    for sig, dstb in enumerate([vb, x1b, x2b]):
                        for k in range(4):
                            steps.append(lambda sig=sig, dstb=dstb, k=k: sc_tile(seg, sig, dstb, k))
                    return steps

                def sc_tile(seg, sig, dstb, k, b=b):
                    if True:
                        dperm = dstb[:, :].rearrange("p (b a) -> p a b", a=64)
                        if True:
                            bank = nb(C)
                            key = "ps%d" % bank
                            for j in range(3):
                                mm(P, C.ps[bank][:, :], dgv[:, sig * 3 + j, :], pre[:, sig, seg, 512 * k + j:512 * k + j + 512], j == 0, j == 2, ["dg", "pre%d" % seg], [key])
                            dview = dperm[:, seg * 32 + 8 * k:seg * 32 + 8 * k + 8, :]
                            pview = C.ps[bank][:, :].rearrange("p (a b) -> p a b", b=64)
                            sel = (sig * 4 + k) % 3
                            if sel == 0:
                                P.op("scalar", lambda e, dview=dview, pview=pview, sig=sig: e.activation(
                                    out=dview, in_=pview, func=AF.Identity, bias=cb[:, b * 3 + sig:b * 3 + sig + 1]),
                                    reads=[key, "cb"], writes=["sig%d" % sig])
                            elif sel == 1:
                                P.op("vector", lambda e, dview=dview, pview=pview, sig=sig: e.tensor_scalar(
                                    out=dview, in0=pview, scalar1=cb[:, b * 3 + sig:b * 3 + sig + 1], scalar2=None, op0=ALU.add),
                                    reads=[key, "cb"], writes=["sig%d" % sig])
                            else:
                                C.stgi = getattr(C, "stgi", 0) + 1
                                sg_, sk_ = stg_[C.stgi % 2], "stg%d" % (C.stgi % 2)
                                if C.stgi % 2 == 0:
                                    P.op("scalar", lambda e, sg_=sg_, bank=bank, sig=sig: e.activation(
                                        out=sg_[:, :], in_=C.ps[bank][:, :], func=AF.Identity, bias=cb[:, b * 3 + sig:b * 3 + sig + 1]),
                                        reads=[key, "cb"], writes=[sk_])
                                else:
                                    P.op("vector", lambda e, sg_=sg_, bank=bank, sig=sig: e.tensor_scalar(
                                        out=sg_[:, :], in0=C.ps[bank][:, :], scalar1=cb[:, b * 3 + sig:b * 3 + sig + 1], scalar2=None, op0=ALU.add),
                                        reads=[key, "cb"], writes=[sk_])
                                P.op("gpsimd", lambda e, sg_=sg_, dview=dview: e.tensor_copy(out=dview, in_=sg_[:, :].rearrange("p (a b) -> p a b", b=64)),
                                     reads=[sk_], writes=["sig%d" % sig])

                tiles = [(0, k, 1 + 512 * k) for k in range(4)] + [(1, 4 + k, 1 + 512 * k) for k in range(4)]
                for ti, (seg, c0, p0) in enumerate(tiles):
                    hb_ = hTt[ti % 3]
                    hkey = "hTt%d" % (ti % 3)
                    hv = hb_.rearrange("p (ch n) -> p ch n", ch=8)
                    ukey = ("uT", "A", "Bs")[ti % 3]
                    P.dma("sync", hkey, hv, hTg(c0), writes=[hkey, ukey])
                    for sig in range(3):
                        bank = nb(C)
                        key = "ps%d" % bank
                        for ch in range(8):
                            mm(P, C.ps[bank][:, :], whv[:, ch, sig * 128:(sig + 1) * 128], hv[:, ch, :], ch == 0, ch == 7, ["whb", hkey, ukey], [key])
                        copy_op(P, alt(ti * 3 + sig), pre[:, sig, seg, p0:p0 + 512], C.ps[bank][:, :], [key], ["pre%d" % seg, "Q1", "Q2"])
                        if ti >= 4:
                            sc0.pop(0)()
                    if ti == 3:
                        sc0 = shortconv_steps(0)
                assert not sc0
                if b + 1 < 4:
                    wload(P, whb[:, :], din["w_hy"], 8, 384, "whb", col0=(b + 1) * 384)
                for f in shortconv_steps(1):
                    f()
                if dbg and "uc" in dbg:
                    for sig, dstb in enumerate([vb, x1b, x2b]):
                        for seg in range(2):
                            P.op("vector", lambda e, seg=seg, dstb=dstb: e.tensor_copy(out=tmp[:, :], in_=dstb[:, seg * 2048:(seg + 1) * 2048]), reads=["sig%d" % sig, "tmp"], writes=["tmp"])
                            P.dma("sync", "dbgs", dbg_d["uc"][(sig * 4 + b) * 128:(sig * 4 + b + 1) * 128, seg * 2048:(seg + 1) * 2048], tmp[:, :], reads=["tmp"], writes=["dbgo"])
                P.op("vector", lambda e: e.memset(fz[:, :], 0.0), reads=[], writes=["fz", "Q1", "Q2", "pre0", "pre1"])
                if b == 1:
                    wload(P, wqkv[:, :], din["w_qkv"], 8, 768, "wqkv")
                plist = [((0, 0), (0, 1), None, b), ((0, 1), (1, 0), 1, b), ((1, 0), (1, 1), None, b)]
                if b + 1 < 4:
                    plist.append(((1, 1), (0, 0), 0, b + 1))
                for (d_, k_, gen, gb) in plist:
                    Dl, Kl = Dst(*d_), Kst(*k_)
                    Dl[0]()
                    if gen is not None:
                        kgen(gen, b=gb)
                    Dl[1]()
                    Kl[0]()
                    Dl[2]()
                    Kl[1]()
                    Dl[3]()
                    Kl[2]()
                    Dl[4]()
                    Kl[3]()
                if b + 1 >= 4:
                    for f in Dst(1, 1):
                        f()
            if dbg and "hy" in dbg:
                t2 = U[:, 0:4096].bitcast(F32)
                for b in range(4):
                    P.op("vector", lambda e, b=b: e.tensor_copy(out=t2[:, :], in_=hyT[:, b * OWN:(b + 1) * OWN]), reads=["hyT", "t2"], writes=["t2"])
                    P.dma("sync", "dbgs", dbg_d["hy"][b * 128:(b + 1) * 128, :], t2[:, :], reads=["t2"], writes=["dbgo"])
            flush(P)
            hy.close()

        attnT = sb(mid, "attnT", [128, 4 * OWN], BF16)
        wg = sb(mid, "wg", [128, 8 * 2048], BF16)
        wap = sb(mid, "wap", [128, 4 * 1024], BF16)
        whp = sb(mid, "whp", [128, 4 * 1024], BF16)
        wout = sb(mid, "wout", [128, 8 * 1024], BF16)
        P.nobarrier = {"wg", "wap", "whp", "wout", "wqkv", "whb"}
        attnTv = attnT[:, :].rearrange("p (k n) -> p k n", k=4)
        with contextlib.ExitStack() as ph:
            wqv = wqkv[:, :].rearrange("p (ch n) -> p ch n", ch=8)
            qTz = sb(ph, "qTz", [128, 8 * OWN], BF16)
            qTzv = qTz[:, :].rearrange("p (hp two n) -> p hp two n", two=2, n=OWN)
            qTh = qTz[:, :].rearrange("p (h n) -> p h n", h=8)
            kT = sb(ph, "kT", [128, 2 * EXT], BF16)
            kTv = kT[:, :].rearrange("p (g n) -> p g n", g=2)
            V1 = sb(ph, "V1", [128, 18 * 2 * 65], BF16)
            V1v = V1[:, :].rearrange("p (t g d) -> p t g d", t=18, g=2)
            esb = sb(ph, "esb", [128, 8], F32)
            gq = sb(ph, "gq", [128, 2], F32)
            hTa = [sb(ph, "a_hT%d" % i, [128, 8 * 512], BF16) for i in range(2)]
            sq_ = [sb(ph, "a_sq%d" % i, [128, 640], F32) for i in range(3)]
            st_ = [sb(ph, "a_st%d" % i, [128, 32], F32) for i in range(3)]
            qn_ = [sb(ph, "a_qn%d" % i, [128, 512], BF16) for i in range(2)]
            knd_ = [sb(ph, "a_knd%d" % i, [128, 256], BF16) for i in range(2)]
            pT = [sb(ph, "a_pT%d" % i, [128, 512], BF16) for i in range(6)]
            den_ = [sb(ph, "a_den%d" % i, [128, 8], F32) for i in range(2)]
            on_ = [sb(ph, "a_on%d" % i, [128, 256], BF16) for i in range(2)]
            biasT = sb(ph, "biasT", [128, 5120], BF16)
            for i in range(3):
                P.dma("sync", "biasT", biasT[:, i * 2048:min(5120, (i + 1) * 2048)], din["t_biasT"][:, i * 2048:min(5120, (i + 1) * 2048)], writes=["biasT"])
            P.dma("sync", "esb", esb[:, :], din["sink"].partition_broadcast(128), writes=["esb"])
            P.op("scalar", lambda e: e.activation(out=esb[:, :], in_=esb[:, :], func=AF.Exp), reads=["esb"], writes=["esb"])
            P.dma("sync", "gq", gq[:, 0:1], din["gq"][:, :], writes=["gq"])
            P.dma("sync", "gq", gq[:, 1:2], din["gk"][:, :], writes=["gq"])
            P.op("vector", lambda e: e.tensor_scalar(out=gq[:, 0:1], in0=gq[:, 0:1], scalar1=0.125, scalar2=None, op0=ALU.mult), reads=["gq"], writes=["gq"])
            P.op("vector", lambda e: e.memset(V1[:, :], 1.0), writes=["V1"])

            a_order = list(range(1, 17)) + [0, 17]

            def a1A(j):
                i = a_order[j]
                ck, ci = j // 4, j % 4
                hkey = "a_hT%d" % (ck % 2)
                hv = hTa[ck % 2][:, :].rearrange("p (ch n) -> p ch n", ch=8)
                if ci == 0:
                    if ck < 4:
                        P.dma("sync", hkey, hv, hTg(ck), writes=[hkey])
                    else:
                        P.dma("sync", hkey, hv[:, :, 0:256], hTg(8)[:, :, 0:256], writes=[hkey])
                isq = 1 <= i <= 16
                par = j % 3
                pq, pkv = C.ps[par * 2], C.ps[par * 2 + 1]
                kq, kkv = "ps%d" % (par * 2), "ps%d" % (par * 2 + 1)
                sq, st8 = sq_[par], st_[par]
                if isq:
                    for ch in range(8):
                        mm(P, pq[:, :], hv[:, ch, ci * 128:(ci + 1) * 128], wqv[:, ch, 0:512], ch == 0, ch == 7, [hkey, "wqkv"], [kq])
                for ch in range(8):
                    mm(P, pkv[:, 0:256], hv[:, ch, ci * 128:(ci + 1) * 128], wqv[:, ch, 512:768], ch == 0, ch == 7, [hkey, "wqkv"], [kkv])
                P.op("scalar", lambda e: e.activation(out=sq[:, 512:640], in_=pkv[:, 0:128], func=AF.Square), reads=[kkv], writes=["sqk%d" % par])
                P.op("vector", lambda e: e.tensor_reduce(out=st8[:, 8:10], in_=sq[:, 512:640].rearrange("p (h d) -> p h d", d=64), axis=AX.X, op=ALU.add),
                     reads=["sqk%d" % par], writes=["ssk%d" % par])
                rms_rstd(P, st8[:, 8:10], st8[:, 18:20], st8[:, 28:30], C.neghalf[:, 0:1].broadcast_to([128, 2]), float(HD), "k%d" % par)
                if isq:
                    P.op("scalar", lambda e: e.activation(out=sq[:, 0:512], in_=pq[:, :], func=AF.Square), reads=[kq], writes=["sqq%d" % par])
                    P.op("vector", lambda e: e.tensor_reduce(out=st8[:, 0:8], in_=sq[:, 0:512].rearrange("p (h d) -> p h d", d=64), axis=AX.X, op=ALU.add),
                         reads=["sqq%d" % par], writes=["ssq%d" % par])
                    rms_rstd(P, st8[:, 0:8], st8[:, 10:18], st8[:, 20:28], C.neghalf[:, 0:1].broadcast_to([128, 8]), float(HD), "q%d" % par)

            def a1B(j):
                i = a_order[j]
                isq = 1 <= i <= 16
                par3 = j % 3
                par = j % 2
                pq, pkv = C.ps[par3 * 2], C.ps[par3 * 2 + 1]
                kq, kkv = "ps%d" % (par3 * 2), "ps%d" % (par3 * 2 + 1)
                st8, qn, knd = st_[par3], qn_[par], knd_[par]
                for dup in range(2):
                    P.op("vector", lambda e, dup=dup: e.tensor_tensor(
                        out=knd[:, :].rearrange("p (g u d) -> p g u d", g=2, u=2)[:, :, dup, :], in0=pkv[:, 0:128].rearrange("p (g d) -> p g d", d=64),
                        in1=st8[:, 28:30].unsqueeze(2).broadcast_to([128, 2, 64]), op=ALU.mult), reads=[kkv, "rstdk%d" % par3], writes=["knd%d" % par])
                P.op("scalar", lambda e: e.activation(out=V1v[:, i, :, 0:64], in_=pkv[:, 128:256].rearrange("p (g d) -> p g d", d=64), func=AF.Copy),
                     reads=[kkv], writes=["V1"])
                if isq:
                    P.op("vector", lambda e: e.tensor_tensor(
                        out=qn[:, :].rearrange("p (h d) -> p h d", d=64), in0=pq[:, :].rearrange("p (h d) -> p h d", d=64),
                        in1=st8[:, 20:28].unsqueeze(2).broadcast_to([128, 8, 64]), op=ALU.mult), reads=[kq, "rstdq%d" % par3], writes=["qn%d" % par])

            def a1C(j):
                i = a_order[j]
                isq = 1 <= i <= 16
                par = j % 2
                qn, knd = qn_[par], knd_[par]
                ptb = C.ps[6][:, :].bitcast(BF16)
                kpt = "ps6"
                for g in range(2):
                    P.op("tensor", lambda e, g=g: e.transpose(out=ptb[:, g * 128:(g + 1) * 128], in_=knd[:, g * 128:(g + 1) * 128], identity=C.ident[:, :]),
                         reads=["knd%d" % par], writes=[kpt])
                P.op("scalar", lambda e: e.activation(out=kTv[:, :, i * 128:(i + 1) * 128], in_=ptb[:, 0:256].rearrange("p (g n) -> p g n", g=2),
                                                      func=AF.Copy, scale=gq[:, 1:2]), reads=[kpt, "gq"], writes=["kT"])
                if isq:
                    ptq = C.ps[7][:, :].bitcast(BF16)
                    kptq = "ps7"
                    for hp in range(4):
                        P.op("tensor", lambda e, hp=hp: e.transpose(out=ptq[:, hp * 128:(hp + 1) * 128], in_=qn[:, hp * 128:(hp + 1) * 128], identity=C.ident[:, :]),
                             reads=["qn%d" % par], writes=[kptq])
                    for two in range(2):
                        zv = qTzv[(1 - two) * 64:(2 - two) * 64, :, two, (i - 1) * 128:i * 128]
                        P.op("gpsimd", lambda e, zv=zv: e.memset(zv, 0.0), writes=["qT"])
                    for two in range(2):
                        eng = "scalar" if two == 0 else "vector"
                        src = ptq[two * 64:(two + 1) * 64, 0:512].rearrange("p (k n) -> p k n", k=4)
                        dstv = qTzv[two * 64:(two + 1) * 64, :, two, (i - 1) * 128:i * 128]
                        if eng == "scalar":
                            P.op("scalar", lambda e, src=src, dstv=dstv, two=two: e.activation(out=dstv, in_=src, func=AF.Copy, scale=gq[two * 64:(two + 1) * 64, 0:1]),
                                 reads=[kptq, "gq"], writes=["qT"])
                        else:
                            P.op("vector", lambda e, src=src, dstv=dstv, two=two: e.tensor_scalar(out=dstv, in0=src, scalar1=gq[two * 64:(two + 1) * 64, 0:1], scalar2=None, op0=ALU.mult),
                                 reads=[kptq, "gq"], writes=["qT"])

            a1A(0)
            a1A(1)
            for i in range(18):
                if i + 2 < 18:
                    a1A(i + 2)
                a1B(i)
                a1C(i)
            wload(P, wg[:, :], din["w_g"], 8, 2048, "wg")
            wload(P, wap[:, :], din["w_ap"], 4, 1024, "wap")
            wload(P, whp[:, :], din["w_hp"], 4, 1024, "whp")
            wload(P, wout[:, :], din["w_out"], 8, 1024, "wout")

            def a2S(it):
                n, g = it // 2, it % 2
                for reli, rel in enumerate((-1, 0, 1)):
                    kt = n + 1 + rel
                    kind = reli
                    if n == 0 and rel == -1:
                        kind = 3
                    if n == 15 and rel == 1:
                        kind = 4
                    bank = (it % 2) * 3 + reli
                    key = "ps%d" % bank
                    mm(P, C.ps[bank][:, :], C.ident[:, :], biasT[:, (kind * 8 + 4 * g) * 128:(kind * 8 + 4 * g + 4) * 128],
                       True, False, ["biasT", "identb"], [key])
                    for r in range(4):
                        h = 4 * g + r
                        mm(P, C.ps[bank][:, r * 128:(r + 1) * 128], kTv[:, g, kt * 128:(kt + 1) * 128], qTh[:, h, n * 128:(n + 1) * 128], False, r == 3, ["kT", "qT"], [key])
                    pt = pT[(it % 2) * 3 + reli]
                    pk = "pT%d" % ((it % 2) * 3 + reli)
                    P.op("scalar", lambda e, pt=pt, bank=bank: e.activation(out=pt[:, :], in_=C.ps[bank][:, :], func=AF.Exp), reads=[key], writes=[pk])

            def a2PV(it):
                n, g = it // 2, it % 2
                po = C.ps[6]
                den, on = den_[it % 2], on_[it % 2]
                for r in range(4):
                    for reli in range(3):
                        kt = n + reli
                        pt = pT[(it % 2) * 3 + reli]
                        pk = "pT%d" % ((it % 2) * 3 + reli)
                        mm(P, po[:, r * 65:(r + 1) * 65], pt[:, r * 128:(r + 1) * 128], V1v[:, kt, g, :], reli == 0, reli == 2, [pk, "V1"], ["ps6"])
                pov = po[:, 0:260].rearrange("p (r d) -> p r d", d=65)
                P.op("vector", lambda e: e.tensor_tensor(out=den[:, 0:4], in0=pov[:, :, 64], in1=esb[:, 4 * g:4 * g + 4], op=ALU.add),
                     reads=["ps6", "esb"], writes=["den%d" % (it % 2)])
                P.op("vector", lambda e: e.reciprocal(out=den[:, 4:8], in_=den[:, 0:4]), reads=["den%d" % (it % 2)], writes=["rden%d" % (it % 2)])
                P.op("vector", lambda e: e.tensor_tensor(out=on[:, :].rearrange("p (r d) -> p r d", d=64), in0=pov[:, :, 0:64],
                                                         in1=den[:, 4:8].unsqueeze(2).broadcast_to([128, 4, 64]), op=ALU.mult),
                     reads=["ps6", "rden%d" % (it % 2)], writes=["on%d" % (it % 2)])

            def a2T(it):
                n, g = it // 2, it % 2
                on = on_[it % 2]
                ptb = C.ps[7][:, :].bitcast(BF16)
                for j in range(2):
                    P.op("tensor", lambda e, j=j: e.transpose(out=ptb[:, j * 128:(j + 1) * 128], in_=on[:, j * 128:(j + 1) * 128], identity=C.ident[:, :]),
                         reads=["on%d" % (it % 2)], writes=["ps7"])
                P.op("scalar", lambda e: e.activation(out=attnTv[:, 2 * g:2 * g + 2, n * 128:(n + 1) * 128],
                                                      in_=ptb[:, 0:256].rearrange("p (k n) -> p k n", k=2), func=AF.Copy),
                     reads=["ps7"], writes=["attnT"])

            a2S(0)
            for it in range(32):
                if it + 1 < 32:
                    a2S(it + 1)
                a2PV(it)
                if it >= 1:
                    a2T(it - 1)
            a2T(31)
            if dbg and "attn" in dbg:
                t2 = sb(ph, "dbg_a", [128, OWN], F32)
                for k in range(4):
                    P.op("vector", lambda e, k=k: e.tensor_copy(out=t2[:, :], in_=attnTv[:, k, :]), reads=["attnT", "t2"], writes=["t2"])
                    P.dma("sync", "dbgs", dbg_d["attn"][k * 128:(k + 1) * 128, :], t2[:, :], reads=["t2"], writes=["dbgo"])
            flush(P)

        with contextlib.ExitStack() as ph:
            wgv = wg[:, :].rearrange("p (ch n) -> p ch n", ch=8)
            wapv = wap[:, :].rearrange("p (ch n) -> p ch n", ch=4)
            whpv = whp[:, :].rearrange("p (ch n) -> p ch n", ch=4)
            woutv = wout[:, :].rearrange("p (ch n) -> p ch n", ch=8)
            hTm = [sb(ph, "m_hT%d" % i, [128, 8 * 512], BF16) for i in range(2)]
            sa_ = [sb(ph, "m_sa%d" % i, [128, 512], F32) for i in range(2)]
            sh_ = [sb(ph, "m_sh%d" % i, [128, 512], F32) for i in range(2)]
            t1_ = [sb(ph, "m_t1%d" % i, [128, 512], F32) for i in range(2)]
            t2_ = [sb(ph, "m_t2%d" % i, [128, 512], F32) for i in range(2)]
            mixT = sb(ph, "m_mix", [128, 8 * 512], BF16)
            mixv = mixT[:, :].rearrange("p (ch n) -> p ch n", ch=8)
            xt = [sb(ph, "m_xt%d" % i, [128, D], F32) for i in range(2)]
            x1t = [sb(ph, "m_x1t%d" % i, [128, D], F32) for i in range(2)]
            xn2 = [sb(ph, "m_xn%d" % i, [128, D], F32) for i in range(2)]
            h2s = [sb(ph, "m_h2s%d" % i, [128, 8 * 512], BF16) for i in range(2)]
            ss = sb(ph, "m_ss", [128, 16], F32)
            ms = sb(ph, "m_ms", [128, 16], F32)
            rstd = sb(ph, "m_rstd", [128, 16], F32)
            for tt in range(4):
                hb_ = hTm[tt % 2]
                hkey = "m_hT%d" % (tt % 2)
                hv = hb_[:, :].rearrange("p (ch n) -> p ch n", ch=8)
                P.dma("sync", hkey, hv, hTg(tt), writes=[hkey])
                tsl = slice(tt * 512, (tt + 1) * 512)
                for m in range(8):
                    msl = slice(m * 128, (m + 1) * 128)
                    pb = (m % 2) * 4
                    sa, sh, t1, t2m = sa_[m % 2], sh_[m % 2], t1_[m % 2], t2_[m % 2]
                    ksa, ksh, kt1, kt2 = "sa%d" % (m % 2), "sh%d" % (m % 2), "t1%d" % (m % 2), "t2%d" % (m % 2)
                    kb = ["ps%d" % (pb + j) for j in range(4)]
                    for ch in range(8):
                        mm(P, C.ps[pb][:, :], wgv[:, ch, msl], hv[:, ch, :], ch == 0, ch == 7, ["wg", hkey], [kb[0]])
                    P.op("scalar", lambda e, sa=sa, pb=pb: e.activation(out=sa[:, :], in_=C.ps[pb][:, :], func=AF.Sigmoid), reads=[kb[0]], writes=[ksa])
                    for ch in range(8):
                        mm(P, C.ps[pb + 1][:, :], wgv[:, ch, 1024 + m * 128:1024 + (m + 1) * 128], hv[:, ch, :], ch == 0, ch == 7, ["wg", hkey], [kb[1]])
                    P.op("scalar", lambda e, sh=sh, pb=pb: e.activation(out=sh[:, :], in_=C.ps[pb + 1][:, :], func=AF.Sigmoid), reads=[kb[1]], writes=[ksh])
                    for k in range(4):
                        mm(P, C.ps[pb + 2][:, :], wapv[:, k, msl], attnTv[:, k, tsl], k == 0, k == 3, ["wap", "attnT"], [kb[2]])
                    for k in range(4):
                        mm(P, C.ps[pb + 3][:, :], whpv[:, k, msl], hyTv[:, k, tsl], k == 0, k == 3, ["whp", "hyT"], [kb[3]])
                    P.op("vector", lambda e, t1=t1, sa=sa, pb=pb: e.tensor_tensor(out=t1[:, :], in0=C.ps[pb + 2][:, :], in1=sa[:, :], op=ALU.mult), reads=[kb[2], ksa], writes=[kt1])
                    P.op("vector", lambda e, t2m=t2m, sh=sh, pb=pb: e.tensor_tensor(out=t2m[:, :], in0=C.ps[pb + 3][:, :], in1=sh[:, :], op=ALU.mult), reads=[kb[3], ksh], writes=[kt2])
                    P.op("gpsimd", lambda e, m=m, t1=t1, t2m=t2m: e.tensor_tensor(out=mixv[:, m, :], in0=t1[:, :], in1=t2m[:, :], op=ALU.add), reads=[kt1, kt2], writes=["mix"])
                h2b = h2s[tt % 2]
                h2key = "h2s%d" % (tt % 2)

                def mA(s, tt=tt):
                    idx = tt * 4 + s
                    xb, xkey = xt[idx % 2], "m_xt%d" % (idx % 2)
                    x1b_, x1key = x1t[idx % 2], "m_x1t%d" % (idx % 2)
                    r0 = idx * 128
                    P.dma("sync", xkey, xb[:, :], din["x_in"][r0:r0 + 128, :], writes=[xkey])
                    for nh in range(2):
                        bank = 4 + nh
                        key = "ps%d" % bank
                        for m in range(8):
                            mm(P, C.ps[bank][:, :], mixv[:, m, s * 128:(s + 1) * 128], woutv[:, m, nh * 512:(nh + 1) * 512], m == 0, m == 7, ["mix", "wout"], [key])
                        P.op("vector", lambda e, nh=nh, bank=bank, xb=xb, x1b_=x1b_: e.tensor_tensor(
                            out=x1b_[:, nh * 512:(nh + 1) * 512], in0=C.ps[bank][:, :], in1=xb[:, nh * 512:(nh + 1) * 512], op=ALU.add),
                            reads=[key, xkey], writes=[x1key])
                    P.dma("gpsimd", "x1st%d" % (idx % 2), x1_d[idx * 128:(idx + 1) * 128, :], x1b_[:, :], reads=[x1key], writes=["x1d"])
                    norm_stats(P, C, x1b_, x1key, xn2[idx % 2], ss, ms, rstd, idx, "m_xn%d" % (idx % 2), junkkey="m_xn%d" % (idx % 2))

                def mB(s, tt=tt, h2b=h2b, h2key=h2key):
                    idx = tt * 4 + s
                    dv = h2b[:, :].rearrange("p (ch n) -> p ch n", ch=8)[:, :, s * 128:(s + 1) * 128]
                    norm_finish(P, C, x1t[idx % 2], "m_x1t%d" % (idx % 2), xn2[idx % 2], "m_xn%d" % (idx % 2), rstd, idx, g2, dv, h2key, [6, 7])

                mA(0)
                for s in range(4):
                    if s + 1 < 4:
                        mA(s + 1)
                    mB(s)
                P.dma("gpsimd", "h2st%d" % (tt % 2), hT2v[:, :, tsl], h2b[:, :].rearrange("p (ch n) -> p ch n", ch=8), reads=[h2key], writes=["hT2d"])
            if dbg and "x1" in dbg:
                barrier(P)
                for idx in range(16):
                    P.dma("sync", "dbgl", xt[0][:, :], x1_d[idx * 128:(idx + 1) * 128, :], writes=["m_xt0"])
                    P.dma("sync", "dbgs", dbg_d["x1"][idx * 128:(idx + 1) * 128, :], xt[0][:, :], reads=["m_xt0"], writes=["dbgo"])
            flush(P)

        mid.close()
        with contextlib.ExitStack() as ph:
            actT = sb(ph, "actT", [128, NFC * OWN], BF16)
            actv = actT[:, :].rearrange("p (k n) -> p k n", k=NFC)
            wdn = sb(ph, "wdn", [128, NFC * D], BF16)
            wdnv = wdn[:, :].rearrange("p (k n) -> p k n", k=NFC)
            h2a = sb(ph, "h2a", [128, 8 * OWN], BF16)
            h2v = h2a[:, :].rearrange("p (ch n) -> p ch n", ch=8)
            wgu = [sb(ph, "wgu%d" % i, [128, 8 * 256], BF16) for i in range(2)]
            sg = [sb(ph, "f_sg%d" % i, [128, 512], F32) for i in range(2)]
            xr = [sb(ph, "f_xr%d" % i, [128, D], F32) for i in range(2)]
            ot = [sb(ph, "f_ot%d" % i, [128, D], F32) for i in range(2)]
            for i in range(4):
                P.dma("sync", "h2a%d" % i, h2v[:, :, i * 512:(i + 1) * 512], hT2v[:, :, i * 512:(i + 1) * 512], writes=["h2a%d" % i])
            for k in range(NFC):
                wb = wgu[k % 2]
                wkey = "wgu%d" % (k % 2)
                wv = wb[:, :].rearrange("p (ch n) -> p ch n", ch=8)
                P.dma("gpsimd", wkey, wv[:, :, 0:128], din["w_gate"].rearrange("(ch p) n -> p ch n", p=128)[:, :, k * 128:(k + 1) * 128], writes=[wkey])
                P.dma("gpsimd", wkey, wv[:, :, 128:256], din["w_up"].rearrange("(ch p) n -> p ch n", p=128)[:, :, k * 128:(k + 1) * 128], writes=[wkey])
                if k == 1:
                    wload(P, wdn[:, :], din["w_down"], NFC, D, "wdn")
                for tt in range(4):
                    it = k * 4 + tt
                    bg, bu = (it % 2) * 2, (it % 2) * 2 + 1
                    for ch in range(8):
                        mm(P, C.ps[bg][:, :], wv[:, ch, 0:128], h2v[:, ch, tt * 512:(tt + 1) * 512], ch == 0, ch == 7, [wkey, "h2a%d" % tt], ["ps%d" % bg])
                    for ch in range(8):
                        mm(P, C.ps[bu][:, :], wv[:, ch, 128:256], h2v[:, ch, tt * 512:(tt + 1) * 512], ch == 0, ch == 7, [wkey, "h2a%d" % tt], ["ps%d" % bu])
                    sgt, sgk = sg[it % 2], "sg%d" % (it % 2)
                    P.op("scalar", lambda e, sgt=sgt, bg=bg: e.activation(out=sgt[:, :], in_=C.ps[bg][:, :], func=AF.Silu), reads=["ps%d" % bg], writes=[sgk])
                    P.op("vector", lambda e, sgt=sgt, bu=bu, k=k, tt=tt: e.tensor_tensor(
                        out=actv[:, k, tt * 512:(tt + 1) * 512], in0=C.ps[bu][:, :], in1=sgt[:, :], op=ALU.mult), reads=["ps%d" % bu, sgk], writes=["actT"])
            for s in range(16):
                xb, xkey = xr[s % 2], "f_xr%d" % (s % 2)
                ob, okey = ot[s % 2], "f_ot%d" % (s % 2)
                P.dma("sync", xkey, xb[:, :], x1_d[s * 128:(s + 1) * 128, :], writes=[xkey])
                for nh in range(2):
                    bank = 4 + (s % 2) * 2 + nh
                    key = "ps%d" % bank
                    for k in range(NFC):
                        mm(P, C.ps[bank][:, :], actv[:, k, s * 128:(s + 1) * 128], wdnv[:, k, nh * 512:(nh + 1) * 512], k == 0, k == NFC - 1, ["actT", "wdn"], [key])
                    P.op("vector", lambda e, nh=nh, bank=bank, xb=xb, ob=ob: e.tensor_tensor(
                        out=ob[:, nh * 512:(nh + 1) * 512], in0=C.ps[bank][:, :], in1=xb[:, nh * 512:(nh + 1) * 512], op=ALU.add),
                        reads=[key, xkey], writes=[okey])
                P.dma("sync", "ost%d" % (s % 2), out_d[s * 128:(s + 1) * 128, :], ob[:, :], reads=[okey], writes=["outd"])
            flush(P)
    return nc


_CACHE = {}


def kernel(**inputs):
    ins = [host_inputs(inputs, c) for c in range(8)]
    shapes = {k: v.shape for k, v in ins[0].items()}
    if "nc" not in _CACHE:
        _CACHE["nc"] = build_program(shapes)
    res = run_bass_kernel_spmd(_CACHE["nc"], ins, core_ids=list(range(8)))
    x = np.asarray(inputs["x"])
    out = np.zeros(x.shape, np.float32)
    for c in range(8):
        b, par = c // 2, c % 2
        out[b, par * OWN:(par + 1) * OWN] = res.results[c]["out"]
    return out
```

```python
import contextlib
import math
import numpy as np
import concourse.bass as bass
import concourse.mybir as mybir
from concourse.bass_utils import run_bass_kernel_spmd

F32 = mybir.dt.float32
BF16 = mybir.dt.bfloat16
AF = mybir.ActivationFunctionType
ALU = mybir.AluOpType
AX = mybir.AxisListType

NFFT = 8192
L = 4096


class Prog:
    ENGS = ["sync", "scalar", "vector", "gpsimd", "tensor"]

    def __init__(self, nc, stack):
        self.nc = nc
        self.stack = stack
        self.ops = {e: [] for e in self.ENGS}
        self.tick = {e: 0 for e in self.ENGS}
        self.esem = {e: stack.enter_context(nc.semaphore("es_" + e)) for e in self.ENGS}
        self.waited = {e: {} for e in self.ENGS}
        self.lastw = {}
        self.readers = {}
        self.dsem = {}
        self.same_engine_sync = True

    def _need(self, e, ev, waits):
        if ev is None:
            return
        if ev[0] == "e":
            if ev[1] == e and (not self.same_engine_sync or e == "tensor"):
                return
            k = "e_" + ev[1]
            sem = self.esem[ev[1]]
        else:
            k = "d_" + ev[1]
            sem = self.dsem[ev[1]][0]
        if self.waited[e].get(k, 0) >= ev[2]:
            return
        cur = waits.get(k)
        if cur is None or cur[1] < ev[2]:
            waits[k] = (sem, ev[2])

    def _deps(self, e, reads, writes):
        waits = {}
        for k in reads:
            self._need(e, self.lastw.get(k), waits)
        for k in writes:
            self._need(e, self.lastw.get(k), waits)
            for ev in self.readers.get(k, []):
                self._need(e, ev, waits)
        for k, (sem, v) in waits.items():
            self.waited[e][k] = v
        return list(waits.values())

    def _commit(self, ev, reads, writes):
        for k in reads:
            lst = self.readers.setdefault(k, [])
            lst[:] = [x for x in lst if not (x[0] == ev[0] and x[1] == ev[1])]
            lst.append(ev)
        for k in writes:
            self.lastw[k] = ev
            self.readers[k] = []

    def op(self, e, fn, reads=(), writes=()):
        waits = self._deps(e, reads, writes)
        self.tick[e] += 1
        t = self.tick[e]
        sem = self.esem[e]
        eng = getattr(self.nc, e)

        def run():
            for (s, v) in waits:
                eng.wait_ge(s, v)
            fn(eng).then_inc(sem, 1)
        self.ops[e].append(run)
        self._commit(("e", e, t), reads, writes)

    def dma(self, q, slot, out, in_, reads=(), writes=(), **kw):
        if slot not in self.dsem:
            self.dsem[slot] = [self.stack.enter_context(self.nc.semaphore("ds_" + slot)), 0]
        waits = self._deps(q, reads, writes)
        self.dsem[slot][1] += 16
        cnt = self.dsem[slot][1]
        sem = self.dsem[slot][0]
        eng = getattr(self.nc, q)

        def run():
            for (s, v) in waits:
                eng.wait_ge(s, v)
            eng.dma_start(out=out, in_=in_, **kw).then_inc(sem, 16)
        self.ops[q].append(run)
        self._commit(("d", slot, cnt), reads, writes)

    def finish(self, e, keys):
        waits = self._deps(e, keys, [])
        eng = getattr(self.nc, e)

        def run():
            for (s, v) in waits:
                eng.wait_ge(s, v)
        self.ops[e].append(run)

    def emit(self):
        with self.nc.Block() as block:
            @block.sync
            def _(x):
                for f in self.ops["sync"]:
                    f()

            @block.scalar
            def _(x):
                for f in self.ops["scalar"]:
                    f()

            @block.vector
            def _(x):
                for f in self.ops["vector"]:
                    f()

            @block.gpsimd
            def _(x):
                for f in self.ops["gpsimd"]:
                    f()

            @block.tensor
            def _(x):
                for f in self.ops["tensor"]:
                    f()


def fft_tables():
    N = NFFT
    T = {}
    f1h = np.arange(64) + 0.5
    t1 = np.arange(128)[:, None]
    ang = 2 * np.pi * t1 * f1h[None, :] / 128.0
    F1 = np.zeros((128, 128))
    for h in range(2):
        F1[:, h * 64:h * 64 + 32] = np.cos(ang[:, 32 * h:32 * h + 32])
        F1[:, h * 64 + 32:h * 64 + 64] = -np.sin(ang[:, 32 * h:32 * h + 32])
    T["F1"] = F1
    t2 = np.arange(64)
    f2 = np.arange(64)
    G = np.zeros((128, 32, 2, 128))
    for h in range(2):
        for f1l in range(32):
            f1 = 32 * h + f1l
            th = 2 * np.pi * t2[:, None] * (f1h[f1] + 128 * f2[None, :]) / N
            c, s = np.cos(th), np.sin(th)
            G[h * 64:(h + 1) * 64, f1l, 0, 0:64] = c
            G[h * 64:(h + 1) * 64, f1l, 0, 64:128] = -s
            G[h * 64:(h + 1) * 64, f1l, 1, 0:64] = s
            G[h * 64:(h + 1) * 64, f1l, 1, 64:128] = c
    T["G"] = G.reshape(128, 32 * 2 * 128)
    phi = 2 * np.pi * f2[:, None] * np.arange(64)[None, :] / 64.0
    c, s = np.cos(phi), np.sin(phi)
    Fre = np.concatenate([c, -s], axis=0)
    Fim = np.concatenate([s, c], axis=0)

    def mk(Fo):
        Fr, Fi = Fo[0:64], Fo[64:128]
        return np.concatenate([Fr, -Fr], axis=0), np.concatenate([Fi, Fi], axis=0)
    R1re, R2re = mk(Fre)
    R1im, R2im = mk(Fim)
    T["R"] = np.stack([R1re, R2re, R1im, R2im], axis=1).reshape(128, 4 * 64)
    tau1 = np.arange(64)
    H = np.zeros((128, 64, 64))
    for tau2 in range(64):
        psi = 2 * np.pi * f1h[:, None] * (64 * tau1[None, :] + tau2) / N
        H[0:64, tau2, :] = 2.0 / N * np.cos(psi)
        H[64:128, tau2, :] = -2.0 / N * np.sin(psi)
    T["H"] = H.reshape(128, 64 * 64)
    Pm = np.zeros((128, 128))
    for p in range(128):
        Pm[p, (p + 64) % 128] = 1.0
    T["perm"] = Pm
    T["ident"] = np.eye(128)
    return {k: np.ascontiguousarray(v, dtype=np.float32) for k, v in T.items()}


class Ctx:
    pass


def alt(i):
    return "scalar" if (i % 2 == 0) else "vector"


def copy_op(P, e, out, in_, reads, writes):
    if e == "scalar":
        P.op("scalar", lambda g: g.activation(out=out, in_=in_, func=AF.Copy), reads, writes)
    else:
        P.op(e, lambda g: g.tensor_copy(out=out, in_=in_), reads, writes)


def nb(C):
    C.bank = (getattr(C, "bank", -1) + 1) % 8
    return C.bank


def st_T(P, C, src, srckey, pr0, nt1, perm_src=False):
    uT = C.uT
    if perm_src:
        srcv = src[:, 0:nt1 * 64].rearrange("p (b a) -> p b a", a=nt1)
    else:
        srcv = src[:, 0:nt1 * 64].rearrange("p (a b) -> p b a", b=64)
    for g in range(4):
        bank = nb(C)
        key = "ps%d" % bank
        pb = C.ps[bank][:].bitcast(BF16)
        for s in range(16):
            t2 = g * 16 + s
            P.op("tensor",
                 lambda e, t2=t2, s=s, pb=pb: e.transpose(
                     out=pb[0:nt1, s * 64:(s + 1) * 64], in_=srcv[pr0:pr0 + 64, t2, :],
                     identity=C.ident[pr0:pr0 + 64, pr0:pr0 + 64]),
                 reads=[srckey], writes=[key])
        copy_op(P, alt(g), uT[0:nt1, g * 1024:(g + 1) * 1024], pb[0:nt1, 0:1024], [key], ["uT"])


def st_S1(P, C, nt1, F1t):
    uTv = C.uT[:, :].rearrange("p (b c) -> p b c", c=64)
    Av = C.A[:, :].rearrange("p (k c) -> p k c", c=64)
    for g in range(8):
        bank = nb(C)
        key = "ps%d" % bank
        pa = C.ps[bank]
        pav = pa[:, :].rearrange("p (k s) -> p s k", s=8)
        for s in range(8):
            c = g * 8 + s
            for h in range(2):
                P.op("tensor",
                     lambda e, c=c, s=s, h=h, pav=pav: e.matmul(
                         pav[h * 64:(h + 1) * 64, s, :], uTv[0:nt1, :, c],
                         F1t[0:nt1, h * 64:(h + 1) * 64], start=True, stop=True),
                     reads=["uT"], writes=[key])
        copy_op(P, alt(g), Av[:, :, g * 8:(g + 1) * 8], pa[:, :].rearrange("p (k s) -> p k s", s=8), [key], ["A"])


def st_S2(P, C, evac_kind, Kp, Kpp, Q1=None, Q2=None):
    Av = C.A[:, :].rearrange("p (k c) -> p k c", c=64)
    Gv = C.G[:, :].rearrange("p (f r m) -> p f r m", r=2, m=128)
    for g in range(8):
        bank = nb(C)
        key = "ps%d" % bank
        px = C.ps[bank]
        for s in range(8):
            f1 = g * 8 + s
            h, f1l = f1 // 32, f1 % 32
            for r in range(2):
                P.op("tensor",
                     lambda e, s=s, h=h, f1l=f1l, r=r, px=px: e.matmul(
                         px[:, s * 64:(s + 1) * 64], Gv[h * 64:(h + 1) * 64, f1l, r, :],
                         Av[h * 64:(h + 1) * 64, 32 * r + f1l, :], start=(r == 0), stop=(r == 1)),
                     reads=["A"], writes=[key])
        sl = slice(g * 512, (g + 1) * 512)
        if evac_kind == "kernel":
            copy_op(P, alt(g), Kp[:, sl], px[:, :], [key], ["Kp"])
        else:
            P.op("vector", lambda e, sl=sl, px=px: e.tensor_tensor(out=Q1[:, sl], in0=px[:, :], in1=Kp[:, sl], op=ALU.mult),
                 reads=[key, "Kp"], writes=["Q1"])
            P.op("vector", lambda e, sl=sl, px=px: e.tensor_tensor(out=Q2[:, sl], in0=px[:, :], in1=Kpp[:, sl], op=ALU.mult),
                 reads=[key, "Kpp"], writes=["Q2"])


def st_perm(P, C, Kp, Kpp):
    for g in range(8):
        bank = nb(C)
        key = "ps%d" % bank
        sl = slice(g * 512, (g + 1) * 512)
        P.op("tensor", lambda e, sl=sl, bank=bank: e.matmul(C.ps[bank][:, :], C.perm[:, :], Kp[:, sl], start=True, stop=True),
             reads=["Kp"], writes=[key])
        copy_op(P, alt(g + 1), Kpp[:, sl], C.ps[bank][:, :], [key], ["Kpp"])


def st_I1(P, C, Q1, Q2):
    Q1v = Q1[:, :].rearrange("p (f c) -> p f c", c=64)
    Q2v = Q2[:, :].rearrange("p (f c) -> p f c", c=64)
    Rv = C.R[:, :].rearrange("p (r t) -> p r t", t=64)
    for g in range(8):
        bank = nb(C)
        key = "ps%d" % bank
        pb = C.ps[bank]
        for s in range(8):
            c = g * 8 + s
            for ri in range(2):
                P.op("tensor",
                     lambda e, c=c, s=s, ri=ri, pb=pb: e.matmul(
                         pb[ri * 64:(ri + 1) * 64, s * 64:(s + 1) * 64], Q1v[:, :, c], Rv[:, 2 * ri, :],
                         start=True, stop=False),
                     reads=["Q1"], writes=[key])
                P.op("tensor",
                     lambda e, c=c, s=s, ri=ri, pb=pb: e.matmul(
                         pb[ri * 64:(ri + 1) * 64, s * 64:(s + 1) * 64], Q2v[:, :, c], Rv[:, 2 * ri + 1, :],
                         start=False, stop=True),
                     reads=["Q2"], writes=[key])
        copy_op(P, alt(g), C.Bs[:, g * 512:(g + 1) * 512], pb[:, :], [key], ["Bs"])


def st_I2(P, C, pr0, epilogue):
    Bv = C.Bs[:, :].rearrange("p (c t) -> p c t", t=64)
    Hv = C.H[:, :].rearrange("p (t a) -> p t a", a=64)
    for g in range(8):
        bank = nb(C)
        key = "ps%d" % bank
        py = C.ps[bank]
        for s in range(8):
            tau2 = g * 8 + s
            P.op("tensor",
                 lambda e, tau2=tau2, s=s, py=py: e.matmul(
                     py[pr0:pr0 + 64, s * 64:(s + 1) * 64], Bv[:, :, tau2], Hv[:, tau2, :],
                     start=True, stop=True),
                 reads=["Bs"], writes=[key])
        epilogue(P, g, py[pr0:pr0 + 64, :].rearrange("p (s a) -> p s a", a=64), key)


def fft_fwd(P, C, src, srckey, pr0, nt1, evac_kind, Kp=None, Kpp=None, Q1=None, Q2=None, F1t=None):
    if F1t is None:
        F1t = C.F1
    st_T(P, C, src, srckey, pr0, nt1)
    st_S1(P, C, nt1, F1t)
    st_S2(P, C, evac_kind, Kp, Kpp, Q1, Q2)
    if evac_kind == "kernel":
        st_perm(P, C, Kp, Kpp)


def fft_inv(P, C, Q1, Q2, pr0, epilogue):
    st_I1(P, C, Q1, Q2)
    st_I2(P, C, pr0, epilogue)


def tview(buf, pr0, g):
    return buf[pr0:pr0 + 64, g * 512:(g + 1) * 512].rearrange("p (s a) -> p s a", a=64)


D = 1024
SEQ = 4096
NH = 8
HD = 64
DFF = 2816
NFC = DFF // 128
HW = 512
OWN = 2048
EXT = 2304
NROW = EXT + OWN + 128
EPS = 1e-6
NEG = -1e30


def barrier(P):
    for e in P.ENGS:
        waits = []
        for e2 in P.ENGS:
            if P.tick[e2] > P.waited[e].get("e_" + e2, 0) and (e2 != e or e != "tensor"):
                waits.append((P.esem[e2], P.tick[e2]))
                P.waited[e]["e_" + e2] = P.tick[e2]
        for slot, (sem, cnt) in P.dsem.items():
            if slot in getattr(P, "nobarrier", ()):
                continue
            if cnt > P.waited[e].get("d_" + slot, 0):
                waits.append((sem, cnt))
                P.waited[e]["d_" + slot] = cnt
        eng = getattr(P.nc, e)

        def run(waits=waits, eng=eng):
            for (s, v) in waits:
                eng.wait_ge(s, v)
        P.ops[e].append(run)
    keep = {k: v for k, v in P.lastw.items() if v[0] == "d" and v[1] in getattr(P, "nobarrier", ())}
    P.lastw.clear()
    P.lastw.update(keep)
    P.readers.clear()


def mm(P, out, lhsT, rhs, start, stop, reads, writes):
    P.op("tensor", lambda e: e.matmul(out, lhsT, rhs, start=start, stop=stop), reads, writes)


def host_tables(par):
    T = fft_tables()
    N = NFFT
    f1h = np.arange(64) + 0.5
    t1 = np.arange(64).astype(np.float64)
    t1e = np.where((t1 >= 32) & (par == 1), t1 - 64, t1)[:, None]
    ang = 2 * np.pi * t1e * f1h[None, :] / 128.0
    F1d = np.zeros((128, 128))
    for h in range(2):
        F1d[0:64, h * 64:h * 64 + 32] = np.cos(ang[:, 32 * h:32 * h + 32])
        F1d[0:64, h * 64 + 32:h * 64 + 64] = -np.sin(ang[:, 32 * h:32 * h + 32])
    T["F1d"] = F1d.astype(np.float32)
    tau1 = np.arange(64)
    H = np.zeros((128, 64, 64))
    for tau2 in range(64):
        n = 64.0 * tau1 + tau2
        n = np.where((tau1 >= 32) & (par == 1), n - 4096, n)
        psi = 2 * np.pi * f1h[:, None] * n[None, :] / N
        H[0:64, tau2, :] = 2.0 / N * np.cos(psi)
        H[64:128, tau2, :] = -2.0 / N * np.sin(psi)
    T["H"] = H.reshape(128, 4096).astype(np.float32)
    slot = np.arange(NFFT)
    lag = np.where(slot <= L, slot, NFFT - slot)
    lag = np.minimum(lag, L - 1)
    tt = np.linspace(0.0, 1.0, L, dtype=np.float32).astype(np.float64)[lag]
    w = (2.0 * np.pi * np.arange(L, dtype=np.float32) / L).astype(np.float64)[lag]
    bands = np.linspace(1e-4, 15, 16, dtype=np.float32).astype(np.float64)
    zT = np.concatenate([tt[None, :], np.cos(bands[:, None] * w[None, :]), -np.sin(bands[:, None] * w[None, :])], axis=0)
    T["zT"] = zT.astype(np.float32)
    max_decay = math.log(1e-2) / 0.3
    min_decay = math.log(1e-2) / 1.5
    deltas = np.abs(np.linspace(min_decay, max_decay, HW, dtype=np.float32).astype(np.float64))
    sc = deltas / (L - 1)
    dsc = np.stack([-sc, sc], axis=1)
    dbi = np.zeros((HW, 16))
    for k in range(8):
        dbi[:, k] = -sc * 512 * k
        dbi[:, 8 + k] = -sc * (4096 - 512 * k)
    T["dsc"] = dsc.reshape(4, 128, 2).transpose(1, 0, 2).reshape(128, 8).astype(np.float32)
    T["dbi"] = dbi.reshape(4, 128, 16).transpose(1, 0, 2).reshape(128, 64).astype(np.float32)
    T["iota"] = np.tile(np.arange(512, dtype=np.float32)[None, :], (128, 1))
    slopes = 2.0 ** (-8.0 * np.arange(1, 9) / 8.0)
    s = np.arange(128)[:, None]
    q = np.arange(128)[None, :]
    bt = np.zeros((128, 5, 8, 128))
    for kind in range(5):
        rel = [-1, 0, 1, -1, 1][kind]
        dist = np.abs(q - (s + 128 * rel))
        for h in range(8):
            b = np.where(dist <= 128, -slopes[h] * dist, NEG)
            if (kind == 3 and par == 0) or (kind == 4 and par == 1):
                b = np.full_like(b, NEG)
            bt[:, kind, h, :] = b
    T["biasT"] = bt.reshape(128, 5 * 8 * 128).astype(np.float32)
    T["ones"] = np.ones((128, 128), np.float32)
    import ml_dtypes
    T["biasT"] = T["biasT"].astype(ml_dtypes.bfloat16)
    return T


def host_inputs(inp, core):
    b, par = core // 2, core % 2
    f = lambda a: np.ascontiguousarray(np.asarray(a), dtype=np.float32)
    x = f(inp["x"])[b]
    t0 = OWN * par
    o0 = OWN * (1 - par)
    xin = np.zeros((NROW, D), np.float32)
    xin[0:OWN] = x[t0:t0 + OWN]
    xin[OWN:2 * OWN] = x[o0:o0 + OWN]
    if t0 - 128 >= 0:
        xin[2 * OWN:2 * OWN + 128] = x[t0 - 128:t0]
    if t0 + OWN + 128 <= SEQ:
        xin[2 * OWN + 128:2 * OWN + 256] = x[t0 + OWN:t0 + OWN + 128]
    hb = 2 * OWN + 256
    for i, t in enumerate([t0 - 1, t0 + OWN, o0 - 1, o0 + OWN]):
        if 0 <= t < SEQ:
            xin[hb + i] = x[t]
    d = {"x_in": xin}
    w_in = f(inp["w_in"])
    d["w_qkv"] = np.ascontiguousarray(w_in[:, 0:768])
    why = w_in[:, 768:2304].reshape(D, 3, 4, 128).transpose(0, 2, 1, 3)
    d["w_hy"] = np.ascontiguousarray(why.reshape(D, 1536))
    d["w_g"] = np.ascontiguousarray(w_in[:, 2304:4352])
    d["g1"] = np.ascontiguousarray(f(inp["norm1_g"]).reshape(8, 128).T)
    d["g2"] = np.ascontiguousarray(f(inp["norm2_g"]).reshape(8, 128).T)
    d["gq"] = np.ascontiguousarray(np.tile(f(inp["q_norm_g"]), 2)[:, None])
    d["gk"] = np.ascontiguousarray(np.tile(f(inp["k_norm_g"]), 2)[:, None])
    d["sink"] = f(inp["attn_sink"])[None, :]
    cw = f(inp["hy_conv_w"]).reshape(3, 3, 4, 128)
    d["cw"] = np.ascontiguousarray(cw.transpose(3, 2, 1, 0).reshape(128, 36))
    cb = f(inp["hy_conv_b"]).reshape(3, 4, 128)
    d["cb"] = np.ascontiguousarray(cb.transpose(2, 1, 0).reshape(128, 12))
    d["fw1"] = f(inp["filt_w1"])
    d["fb1"] = f(inp["filt_b1"])[:, None]
    d["ff1"] = f(inp["filt_freq1"])[:, None]
    d["fw2"] = f(inp["filt_w2"])
    d["fb2"] = f(inp["filt_b2"])[:, None]
    d["ff2"] = f(inp["filt_freq2"])[:, None]
    d["fw3"] = f(inp["filt_w3"])
    dd = f(inp["hy_bias_d"]).reshape(2, 4, 128)
    d["dd"] = np.ascontiguousarray(dd.transpose(2, 1, 0).reshape(128, 8))
    d["w_ap"] = f(inp["w_attn_proj"])
    d["w_hp"] = f(inp["w_hyena_proj"])
    d["w_out"] = f(inp["w_out"])
    d["w_gate"] = f(inp["w_gate"])
    d["w_up"] = f(inp["w_up"])
    d["w_down"] = f(inp["w_down"])
    for k, v in host_tables(par).items():
        d["t_" + k] = v
    return d


IN_SHAPES = None
MAGIC = 12582912.0
PI_SAFE = 3.141592
TWO_PI = 2.0 * math.pi


def flush(P):
    barrier(P)
    P.emit()
    for e in P.ENGS:
        P.ops[e] = []


def wload(P, dst, src, nch, ncols, key, q="gpsimd", col0=0, piece=2048):
    dv = dst.rearrange("p (ch n) -> p ch n", ch=nch)
    sv = src.rearrange("(ch p) n -> p ch n", p=128)
    for c0 in range(0, ncols, piece):
        w = min(piece, ncols - c0)
        P.dma(q, key, dv[:, :, c0:c0 + w], sv[:, :, col0 + c0:col0 + c0 + w], writes=[key])


def rms_rstd(P, ss_ap, ms_ap, rstd_ap, neghalf, n, k):
    P.op("vector", lambda e: e.tensor_scalar(out=ms_ap, in0=ss_ap, scalar1=1.0 / n, scalar2=EPS, op0=ALU.mult, op1=ALU.add),
         reads=["ss" + k], writes=["ms" + k])
    P.op("gpsimd", lambda e: e.tensor_tensor(out=rstd_ap, in0=ms_ap, in1=neghalf, op=ALU.pow),
         reads=["ms" + k], writes=["rstd" + k])


def norm_stats(P, C, xb, xkey, junk, ss, ms, rstd, col, k, junkkey="junk"):
    P.op("scalar", lambda e: e.activation(out=junk[:, :], in_=xb[:, :], func=AF.Square, accum_out=ss[:, col:col + 1]),
         reads=[xkey], writes=[junkkey, "ss" + k])
    rms_rstd(P, ss[:, col:col + 1], ms[:, col:col + 1], rstd[:, col:col + 1], C.neghalf[:, 0:1], float(D), k)


def norm_finish(P, C, xb, xkey, xn, xnkey, rstd, col, g8, dst_view, dstkey, banks):
    k = xnkey
    P.op("scalar", lambda e: e.activation(out=xn[:, :], in_=xb[:, :], func=AF.Copy, scale=rstd[:, col:col + 1]),
         reads=[xkey, "rstd" + k], writes=[xnkey])
    for half in range(2):
        bank = banks[half]
        key = "ps%d" % bank
        for j in range(4):
            ch = half * 4 + j
            P.op("tensor", lambda e, ch=ch, j=j, bank=bank: e.transpose(
                out=C.ps[bank][:, j * 128:(j + 1) * 128], in_=xn[:, ch * 128:(ch + 1) * 128], identity=C.identf[:, :]),
                reads=[xnkey], writes=[key])
        P.op("vector", lambda e, half=half, bank=bank: e.tensor_tensor(
            out=dst_view[:, half * 4:(half + 1) * 4, :], in0=C.ps[bank][:, :].rearrange("p (j t) -> p j t", t=128),
            in1=g8[:, half * 4:(half + 1) * 4].unsqueeze(2).broadcast_to([128, 4, 128]), op=ALU.mult),
            reads=[key], writes=[dstkey])


BF16_INPUTS = ("t_biasT",)


def build_program(shapes, dbg=None):
    nc = bass.Bass("TRN2", target_bir_lowering=False)
    din = {}
    for k, shp in shapes.items():
        din[k] = nc.dram_tensor(k, list(shp), BF16 if k in BF16_INPUTS else F32, kind="ExternalInput").ap()
    out_d = nc.dram_tensor("out", [OWN, D], F32, kind="ExternalOutput").ap()
    hT_d = nc.dram_tensor("hT_d", [9 * 128, 8 * 512], BF16, kind="Internal").ap()
    hT2_d = nc.dram_tensor("hT2_d", [D, OWN], BF16, kind="Internal").ap()
    x1_d = nc.dram_tensor("x1_d", [OWN, D], F32, kind="Internal").ap()
    hTg = lambda g: hT_d[g * 128:(g + 1) * 128, :].rearrange("p (ch n) -> p ch n", ch=8)
    hT2v = hT2_d.rearrange("(ch p) n -> p ch n", p=128)
    dbg_d = {}
    if dbg:
        for k, shp in dbg.items():
            dbg_d[k] = nc.dram_tensor("dbg_" + k, list(shp), F32, kind="ExternalOutput").ap()

    with contextlib.ExitStack() as st:
        P = Prog(nc, st)
        C = Ctx()
        C.ps = [st.enter_context(nc.psum_tensor("ps%d" % i, [128, 512], F32)) for i in range(8)]
        sb = lambda stack, name, shape, dt: stack.enter_context(nc.sbuf_tensor("s_" + name, shape, dt))
        C.identf = sb(st, "identf", [128, 128], F32)
        C.ident = sb(st, "identb", [128, 128], BF16)
        C.onesb = sb(st, "onesb", [128, 128], BF16)
        C.onesf = sb(st, "onesf", [128, 128], F32)
        C.neghalf = sb(st, "neghalf", [128, 1], F32)
        g1 = sb(st, "g1", [128, 8], F32)
        g2 = sb(st, "g2", [128, 8], F32)
        mid = contextlib.ExitStack()
        hyT = sb(mid, "hyT", [128, 4 * OWN], BF16)
        h2T = sb(mid, "h2T", [128, 4096], BF16)
        wqkv = sb(mid, "wqkv", [128, 8 * 768], BF16)
        hyTv = hyT[:, :].rearrange("p (k n) -> p k n", k=4)
        P.dma("sync", "identf", C.identf[:, :], din["t_ident"][:, :], writes=["identf"])
        P.dma("sync", "onesf", C.onesf[:, :], din["t_ones"][:, :], writes=["onesf"])
        P.dma("gpsimd", "identb", C.ident[:, :], din["t_ident"][:, :], writes=["identb"])
        P.dma("gpsimd", "onesb", C.onesb[:, :], din["t_ones"][:, :], writes=["onesb"])
        P.dma("sync", "g1", g1[:, :], din["g1"][:, :], writes=["g1"])
        P.dma("sync", "g2", g2[:, :], din["g2"][:, :], writes=["g2"])
        P.op("vector", lambda e: e.memset(C.neghalf[:, :], -0.5), writes=["neghalf"])
        barrier(P)

        hy = contextlib.ExitStack()
        C.F1 = sb(hy, "F1", [128, 128], BF16)
        C.F1d = sb(hy, "F1d", [128, 128], BF16)
        C.G = sb(hy, "G", [128, 8192], BF16)
        C.R = sb(hy, "R", [128, 256], BF16)
        C.H = sb(hy, "H", [128, 4096], BF16)
        C.perm = sb(hy, "perm", [128, 128], BF16)
        P.dma("gpsimd", "F1", C.F1[:, :], din["t_F1"][:, :], writes=["F1"])
        P.dma("gpsimd", "F1d", C.F1d[:, :], din["t_F1d"][:, :], writes=["F1d"])
        P.dma("gpsimd", "R", C.R[:, :], din["t_R"][:, :], writes=["R"])
        P.dma("gpsimd", "perm", C.perm[:, :], din["t_perm"][:, :], writes=["perm"])
        for i in range(4):
            P.dma("gpsimd", "G", C.G[:, i * 2048:(i + 1) * 2048], din["t_G"][:, i * 2048:(i + 1) * 2048], writes=["G"])
        for i in range(2):
            P.dma("gpsimd", "H", C.H[:, i * 2048:(i + 1) * 2048], din["t_H"][:, i * 2048:(i + 1) * 2048], writes=["H"])
        w3b = sb(hy, "w3b", [128, 2048], BF16)
        for hf in range(2):
            P.dma("gpsimd", "w3b", w3b[hf * 64:(hf + 1) * 64, :], din["fw3"][:, :], writes=["w3b"])
        iota = sb(hy, "iota", [128, 512], F32)
        P.dma("sync", "iota", iota[:, :], din["t_iota"][:, :], writes=["iota"])
        cw = sb(hy, "cw", [128, 36], F32)
        cb = sb(hy, "cb", [128, 12], F32)
        ddt = sb(hy, "ddt", [128, 8], F32)
        dsc = sb(hy, "dsc", [128, 8], F32)
        dbi = sb(hy, "dbi", [128, 64], F32)
        for nm, t in [("cw", cw), ("cb", cb), ("dd", ddt), ("t_dsc", dsc), ("t_dbi", dbi)]:
            P.dma("sync", "c_" + nm, t[:, :], din[nm][:, :], writes=[nm])

        with contextlib.ExitStack() as ph:
            xt = [sb(ph, "p1_xt%d" % i, [128, D], F32) for i in range(6)]
            xn = [sb(ph, "p1_xn%d" % i, [128, D], F32) for i in range(2)]
            junk = sb(ph, "p1_junk", [128, D], BF16)
            hs = [sb(ph, "p1_hs%d" % i, [128, 8 * 512], BF16) for i in range(2)]
            ss = sb(ph, "p1_ss", [128, 40], F32)
            ms = sb(ph, "p1_ms", [128, 40], F32)
            rstd = sb(ph, "p1_rstd", [128, 40], F32)
            zT = sb(ph, "zT", [33, NFFT], F32)
            h1T = sb(ph, "h1T", [128, 4096], F32)
            fw1 = sb(ph, "fw1", [33, 64], F32)
            fw2 = sb(ph, "fw2", [128, 64], F32)
            fv = sb(ph, "fv", [128, 8], F32)
            farg = sb(ph, "farg", [128, 2048], F32)
            ftq = sb(ph, "ftq", [128, 2048], F32)
            for i in range(4):
                P.dma("scalar", "zT", zT[:, i * 2048:(i + 1) * 2048], din["t_zT"][:, i * 2048:(i + 1) * 2048], writes=["zT"])
            P.dma("scalar", "fw1", fw1[:, :], din["fw1"][:, :], writes=["fw1"])
            for hf in range(2):
                P.dma("scalar", "fw2", fw2[hf * 64:(hf + 1) * 64, :], din["fw2"][:, :], writes=["fw2"])
                for j, nm in enumerate(["fb1", "ff1", "fb2", "ff2"]):
                    P.dma("scalar", "fv", fv[hf * 64:(hf + 1) * 64, j:j + 1], din[nm][:, :], writes=["fv"])
            for layer in range(2):
                fc, bc = 1 + 2 * layer, 2 * layer
                P.op("vector", lambda e, layer=layer, fc=fc: e.tensor_scalar(out=fv[:, 4 + 2 * layer:5 + 2 * layer], in0=fv[:, fc:fc + 1], scalar1=1.0 / TWO_PI, scalar2=None, op0=ALU.mult),
                     reads=["fv"], writes=["fv2"])
                P.op("vector", lambda e, layer=layer, bc=bc: e.tensor_tensor(out=fv[:, 5 + 2 * layer:6 + 2 * layer], in0=fv[:, 4 + 2 * layer:5 + 2 * layer], in1=fv[:, bc:bc + 1], op=ALU.mult),
                     reads=["fv", "fv2"], writes=["fv2"])

            def filt_step(layer, sc):
                dst = h1T if layer == 0 else h2T
                dstk = "h1T" if layer == 0 else "h2T"
                for j in range(4):
                    bank = 4 + j
                    key = "ps%d" % bank
                    col = sc * 2048 + j * 512
                    for hf in range(2):
                        if layer == 0:
                            mm(P, C.ps[bank][hf * 64:(hf + 1) * 64, :], fw1[0:33, :], zT[0:33, hf * 4096 + col:hf * 4096 + col + 512], True, True, ["zT", "fw1"], [key])
                        else:
                            mm(P, C.ps[bank][hf * 64:(hf + 1) * 64, :], fw2[hf * 64:(hf + 1) * 64, :], h1T[hf * 64:(hf + 1) * 64, col:col + 512], True, True, ["h1T", "fw2"], [key])
                    P.op("scalar", lambda e, bank=bank, j=j: e.activation(
                        out=farg[:, j * 512:(j + 1) * 512], in_=C.ps[bank][:, :], func=AF.Identity,
                        scale=fv[:, 4 + 2 * layer:5 + 2 * layer], bias=fv[:, 5 + 2 * layer:6 + 2 * layer]),
                        reads=[key, "fv", "fv2"], writes=["farg"])
                P.op("vector", lambda e: e.tensor_scalar(out=ftq[:, :], in0=farg[:, :], scalar1=MAGIC, scalar2=None, op0=ALU.add),
                     reads=["farg"], writes=["ftq"])
                P.op("vector", lambda e: e.scalar_tensor_tensor(out=ftq[:, :], in0=ftq[:, :], scalar=MAGIC, in1=farg[:, :], op0=ALU.subtract, op1=ALU.subtract),
                     reads=["ftq", "farg"], writes=["ftq"])
                P.op("scalar", lambda e: e.activation(out=dst[:, sc * 2048:(sc + 1) * 2048], in_=ftq[:, :], func=AF.Sin, scale=-6.2831845),
                     reads=["ftq"], writes=[dstk])

            filt_sched = {5: (0, 0), 11: (0, 1), 17: (1, 0), 23: (1, 1)}

            ntile = NROW // 128

            def p1_load(i):
                P.dma("sync", "xt%d" % (i % 6), xt[i % 6][:, :], din["x_in"][i * 128:(i + 1) * 128, :], writes=["xt%d" % (i % 6)])

            def p1_stats(i):
                norm_stats(P, C, xt[i % 6], "xt%d" % (i % 6), junk, ss, ms, rstd, i, "p1_%d" % (i % 2))

            for i in range(5):
                p1_load(i)
            p1_stats(0)
            for i in range(ntile):
                if i + 5 < ntile:
                    p1_load(i + 5)
                if i + 1 < ntile:
                    p1_stats(i + 1)
                grp, gi = i // 4, i % 4
                hsb = hs[grp % 2]
                hkey = "hs%d" % (grp % 2)
                dv = hsb[:, :].rearrange("p (ch n) -> p ch n", ch=8)[:, :, gi * 128:(gi + 1) * 128]
                norm_finish(P, C, xt[i % 6], "xt%d" % (i % 6), xn[i % 2], "p1_%d" % (i % 2), rstd, i, g1, dv, hkey,
                            [(i % 2) * 2, (i % 2) * 2 + 1])
                if i in filt_sched:
                    filt_step(*filt_sched[i])
                if gi == 3 or i == ntile - 1:
                    ncol = (gi + 1) * 128
                    P.dma("sync", "hst%d" % (grp % 2), hT_d[grp * 128:(grp + 1) * 128, :], hsb[:, :], reads=[hkey], writes=["hTd"])
            flush(P)
        if dbg and "hT" in dbg:
            with contextlib.ExitStack() as ph:
                t = sb(ph, "dbg_t", [128, 8 * 512], BF16)
                t2 = sb(ph, "dbg_t2", [128, 8 * 512], F32)
                for c0 in range(0, NROW, 512):
                    w = min(512, NROW - c0)
                    P.dma("sync", "dbgl", t[:, :].rearrange("p (ch n) -> p ch n", ch=8)[:, :, 0:w], hTv[:, :, c0:c0 + w], writes=["dt"])
                    P.op("vector", lambda e: e.tensor_copy(out=t2[:, :], in_=t[:, :]), reads=["dt"], writes=["dt2"])
                    P.dma("sync", "dbgs", dbg_d["hT"].rearrange("(ch p) n -> p ch n", p=128)[:, :, c0:c0 + w],
                          t2[:, :].rearrange("p (ch n) -> p ch n", ch=8)[:, :, 0:w], reads=["dt2"], writes=["dbgo"])
                flush(P)

        P.nobarrier = {"whb", "wqkv", "F1", "F1d", "R", "perm", "G", "H", "w3b", "iota", "c_cw", "c_cb", "c_dd", "c_t_dsc", "c_t_dbi"}
        if True:
            ph = hy
            C.uT = sb(ph, "uT", [128, 4096], BF16)
            C.A = sb(ph, "A", [128, 4096], BF16)
            C.Bs = sb(ph, "Bs", [128, 4096], BF16)
            U = sb(ph, "U", [128, 28800], BF16)
            vb = sb(ph, "vb", [128, 4096], BF16)
            x1b = sb(ph, "x1b", [128, 4096], BF16)
            x2b = sb(ph, "x2b", [128, 4096], BF16)
            zb = sb(ph, "zb", [128, 4096], BF16)
            dec_ = [sb(ph, "dec%d" % i, [128, 512], F32) for i in range(2)]
            whb = sb(ph, "whb", [128, 8 * 384], BF16)
            dg = sb(ph, "dg", [128, 9 * 128], BF16)
            fz = sb(ph, "fz", [128, 2], F32)
            stg_ = [sb(ph, "stg%d" % i, [128, 512], BF16) for i in range(2)]
            dgv = dg[:, :].rearrange("p (j n) -> p j n", j=9)
            hh = sb(ph, "hh", [128, 8 * 4], BF16)
            nrm = sb(ph, "nrm", [128, 4], F32)
            pre = U[:, 16384:16384 + 12300].rearrange("p (s g n) -> p s g n", s=3, g=2)
            hTt = [C.uT[:, 0:4096], C.A[:, 0:4096]]
            kern = U[:, 0:8192]
            Kp = U[:, 8192:12288]
            Kpp = U[:, 12288:16384]
            Q1 = U[:, 16384:20480]
            Q2 = U[:, 20480:24576]
            whv = whb[:, :].rearrange("p (ch n) -> p ch n", ch=8)
            hhv = hh[:, :].rearrange("p (ch n) -> p ch n", ch=8)
            for b in range(4):
                def kgen(o, b=b):
                    for ck in range(16):
                        dirn, kk = ck // 8, ck % 8
                        bank = nb(C)
                        key = "ps%d" % bank
                        dec, dkey = dec_[ck % 2], "dec%d" % (ck % 2)
                        col0 = ((dirn * 2 + o) * 512) + b * 128
                        mm(P, C.ps[bank][:, :], w3b[dirn * 64:(dirn + 1) * 64, col0:col0 + 128], h2T[dirn * 64:(dirn + 1) * 64, kk * 512:(kk + 1) * 512], True, True, ["w3b", "h2T"], [key])
                        P.op("scalar", lambda e, dirn=dirn, ck=ck, dec=dec: e.activation(
                            out=dec[:, :], in_=iota[:, :], func=AF.Exp, scale=dsc[:, b * 2 + dirn:b * 2 + dirn + 1], bias=dbi[:, b * 16 + ck:b * 16 + ck + 1]),
                            reads=["iota", "t_dsc", "t_dbi"], writes=[dkey])
                        P.op("vector", lambda e, ck=ck, bank=bank, dirn=dirn, dec=dec: e.scalar_tensor_tensor(
                            out=kern[:, ck * 512:(ck + 1) * 512], in0=C.ps[bank][:, :], scalar=(1.0 if dirn == 0 else -1.0), in1=dec[:, :], op0=ALU.mult, op1=ALU.mult),
                            reads=[key, dkey], writes=["kern"])
                    nk, rk = "nrm%d" % o, "rn%d" % o
                    n0, n1 = nrm[:, 2 * o:2 * o + 1], nrm[:, 2 * o + 1:2 * o + 2]
                    P.op("vector", lambda e: e.memset(kern[:, 4096:4097], 0.0), writes=["kern"])
                    P.op("vector", lambda e: e.tensor_reduce(out=n0, in_=kern[:, :], axis=AX.X, op=ALU.add, apply_absolute_value=True),
                         reads=["kern"], writes=[nk])
                    P.op("vector", lambda e: e.tensor_scalar(out=n0, in0=n0, scalar1=EPS, scalar2=None, op0=ALU.add), reads=[nk], writes=[nk])
                    P.op("vector", lambda e: e.reciprocal(out=n1, in_=n0), reads=[nk], writes=[rk])
                    P.op("vector", lambda e: e.scalar_tensor_tensor(
                        out=kern[:, 0:1], in0=n0, scalar=ddt[:, b * 2 + o:b * 2 + o + 1], in1=kern[:, 0:1], op0=ALU.mult, op1=ALU.add),
                        reads=[nk, "kern", "dd"], writes=["kern"])

                def Kst(o, hb):
                    pr0 = hb * 64
                    return [lambda: st_T(P, C, kern, "kern", pr0, 128),
                            lambda: st_S1(P, C, 128, C.F1),
                            lambda: st_S2(P, C, "kernel", Kp, Kpp),
                            lambda: st_perm(P, C, Kp, Kpp)]

                def Dst(o, hb, b=b):
                    pr0 = hb * 64
                    src, srck = (vb, "sig0") if o == 0 else (zb, "zb")
                    rcol = nrm[pr0:pr0 + 64, 2 * o + 1:2 * o + 2]
                    if o == 0:
                        def epi(P, g, psv, key):
                            P.op("vector", lambda e: e.scalar_tensor_tensor(
                                out=tview(zb, pr0, g), in0=psv, scalar=rcol, in1=tview(x1b, pr0, g), op0=ALU.mult, op1=ALU.mult),
                                reads=[key, "rn0", "sig1"], writes=["zb"])
                    else:
                        def epi(P, g, psv, key):
                            ov = hyT[pr0:pr0 + 64, b * OWN:(b + 1) * OWN].rearrange("p (a t) -> p t a", t=64)[:, g * 8:(g + 1) * 8, :]
                            P.op("vector", lambda e: e.scalar_tensor_tensor(
                                out=ov, in0=psv[:, :, 0:32], scalar=rcol, in1=tview(x2b, pr0, g)[:, :, 0:32], op0=ALU.mult, op1=ALU.mult),
                                reads=[key, "rn1", "sig2"], writes=["hyT"])
                    return [lambda: st_T(P, C, src, srck, pr0, 64, perm_src=True),
                            lambda: st_S1(P, C, 64, C.F1d),
                            lambda: st_S2(P, C, "data", Kp, Kpp, Q1, Q2),
                            lambda: st_I1(P, C, Q1, Q2),
                            lambda: st_I2(P, C, pr0, epi)]

                if b == 0:
                    kgen(0)
                    for f in Kst(0, 0):
                        f()
                if b == 0:
                    wload(P, whb[:, :], din["w_hy"], 8, 384, "whb", col0=0)
                P.dma("sync", "hh", hhv, hTg(8)[:, :, 256:260], writes=["hh"])
                for sig in range(3):
                    bank = nb(C)
                    key = "ps%d" % bank
                    for ch in range(8):
                        mm(P, C.ps[bank][:, 0:4], whv[:, ch, sig * 128:(sig + 1) * 128], hhv[:, ch, :], ch == 0, ch == 7, ["whb", "hh"], [key])
                    for j, (seg, pos) in enumerate([(0, 0), (0, 2049), (1, 0), (1, 2049)]):
                        copy_op(P, alt(j), pre[:, sig, seg, pos:pos + 1], C.ps[bank][:, j:j + 1], [key], ["pre%d" % seg, "Q1", "Q2"])

                for sig in range(3):
                    for j in range(3):
                        wi = (b * 3 + sig) * 3 + j
                        P.op("vector", lambda e, sig=sig, j=j, wi=wi: e.tensor_scalar(out=dgv[:, sig * 3 + j, :], in0=C.ident[:, :], scalar1=cw[:, wi:wi + 1], scalar2=None, op0=ALU.mult),
                             reads=["identb", "cw"], writes=["dg"])

                def shortconv_steps(seg, b=b):
                    steps = []
                    for sig, dstb in enumerate([vb, x1b, x2b]):
                        for k in range(4):
                            steps.append(lambda sig=sig, dstb=dstb, k=k: sc_tile(seg, sig, dstb, k))
                    return steps

                def sc_tile(seg, sig, dstb, k, b=b):
                    if True:
                        dperm = dstb[:, :].rearrange("p (b a) -> p a b", a=64)
                        if True:
                            bank = nb(C)
                            key = "ps%d" % bank
                            for j in range(3):
                                mm(P, C.ps[bank][:, :], dgv[:, sig * 3 + j, :], pre[:, sig, seg, 512 * k + j:512 * k + j + 512], j == 0, j == 2, ["dg", "pre%d" % seg], [key])
                            dview = dperm[:, seg * 32 + 8 * k:seg * 32 + 8 * k + 8, :]
                            pview = C.ps[bank][:, :].rearrange("p (a b) -> p a b", b=64)
                            sel = (sig * 4 + k) % 3
                            if sel == 0:
                                P.op("scalar", lambda e, dview=dview, pview=pview, sig=sig: e.activation(
                                    out=dview, in_=pview, func=AF.Identity, bias=cb[:, b * 3 + sig:b * 3 + sig + 1]),
                                    reads=[key, "cb"], writes=["sig%d" % sig])
                            elif sel == 1:
                                P.op("vector", lambda e, dview=dview, pview=pview, sig=sig: e.tensor_scalar(
                                    out=dview, in0=pview, scalar1=cb[:, b * 3 + sig:b * 3 + sig + 1], scalar2=None, op0=ALU.add),
                                    reads=[key, "cb"], writes=["sig%d" % sig])
                            else:
                                C.stgi = getattr(C, "stgi", 0) + 1
                                sg_, sk_ = stg_[C.stgi % 2], "stg%d" % (C.stgi % 2)
                                if C.stgi % 2 == 0:
                                    P.op("scalar", lambda e, sg_=sg_, bank=bank, sig=sig: e.activation(
                                        out=sg_[:, :], in_=C.ps[bank][:, :], func=AF.Identity, bias=cb[:, b * 3 + sig:b * 3 + sig + 1]),
                                        reads=[key, "cb"], writes=[sk_])
                                else:
                                    P.op("vector", lambda e, sg_=sg_, bank=bank, sig=sig: e.tensor_scalar(
                                        out=sg_[:, :], in0=C.ps[bank][:, :], scalar1=cb[:, b * 3 + sig:b * 3 + sig + 1], scalar2=None, op0=ALU.add),
                                        reads=[key, "cb"], writes=[sk_])
                                P.op("gpsimd", lambda e, sg_=sg_, dview=dview: e.tensor_copy(out=dview, in_=sg_[:, :].rearrange("p (a b) -> p a b", b=64)),
                                     reads=[sk_], writes=["sig%d" % sig])

                tiles = [(0, k, 1 + 512 * k) for k in range(4)] + [(1, 4 + k, 1 + 512 * k) for k in range(4)]
                for ti, (seg, c0, p0) in enumerate(tiles):
                    hb_ = hTt[ti % 2]
                    hkey = "hTt%d" % (ti % 2)
                    hv = hb_.rearrange("p (ch n) -> p ch n", ch=8)
                    ukey = "uT" if ti % 2 == 0 else "A"
                    P.dma("sync", hkey, hv, hTg(c0), writes=[hkey, ukey])
                    for sig in range(3):
                        bank = nb(C)
                        key = "ps%d" % bank
                        for ch in range(8):
                            mm(P, C.ps[bank][:, :], whv[:, ch, sig * 128:(sig + 1) * 128], hv[:, ch, :], ch == 0, ch == 7, ["whb", hkey, ukey], [key])
                        copy_op(P, alt(ti * 3 + sig), pre[:, sig, seg, p0:p0 + 512], C.ps[bank][:, :], [key], ["pre%d" % seg, "Q1", "Q2"])
                        if ti >= 4:
                            sc0.pop(0)()
                    if ti == 3:
                        sc0 = shortconv_steps(0)
                assert not sc0
                if b + 1 < 4:
                    wload(P, whb[:, :], din["w_hy"], 8, 384, "whb", col0=(b + 1) * 384)
                for f in shortconv_steps(1):
                    f()
                if dbg and "uc" in dbg:
                    for sig, dstb in enumerate([vb, x1b, x2b]):
                        for seg in range(2):
                            P.op("vector", lambda e, seg=seg, dstb=dstb: e.tensor_copy(out=tmp[:, :], in_=dstb[:, seg * 2048:(seg + 1) * 2048]), reads=["sig%d" % sig, "tmp"], writes=["tmp"])
                            P.dma("sync", "dbgs", dbg_d["uc"][(sig * 4 + b) * 128:(sig * 4 + b + 1) * 128, seg * 2048:(seg + 1) * 2048], tmp[:, :], reads=["tmp"], writes=["dbgo"])
                P.op("vector", lambda e: e.memset(fz[:, :], 0.0), reads=[], writes=["fz", "Q1", "Q2", "pre0", "pre1"])
                if b == 1:
                    wload(P, wqkv[:, :], din["w_qkv"], 8, 768, "wqkv")
                plist = [((0, 0), (0, 1), None, b), ((0, 1), (1, 0), 1, b), ((1, 0), (1, 1), None, b)]
                if b + 1 < 4:
                    plist.append(((1, 1), (0, 0), 0, b + 1))
                for (d_, k_, gen, gb) in plist:
                    Dl, Kl = Dst(*d_), Kst(*k_)
                    Dl[0]()
                    if gen is not None:
                        kgen(gen, b=gb)
                    Dl[1]()
                    Kl[0]()
                    Dl[2]()
                    Kl[1]()
                    Dl[3]()
                    Kl[2]()
                    Dl[4]()
                    Kl[3]()
                if b + 1 >= 4:
                    for f in Dst(1, 1):
                        f()
            if dbg and "hy" in dbg:
                t2 = U[:, 0:4096].bitcast(F32)
                for b in range(4):
                    P.op("vector", lambda e, b=b: e.tensor_copy(out=t2[:, :], in_=hyT[:, b * OWN:(b + 1) * OWN]), reads=["hyT", "t2"], writes=["t2"])
                    P.dma("sync", "dbgs", dbg_d["hy"][b * 128:(b + 1) * 128, :], t2[:, :], reads=["t2"], writes=["dbgo"])
            flush(P)
            hy.close()

        attnT = sb(mid, "attnT", [128, 4 * OWN], BF16)
        wg = sb(mid, "wg", [128, 8 * 2048], BF16)
        wap = sb(mid, "wap", [128, 4 * 1024], BF16)
        whp = sb(mid, "whp", [128, 4 * 1024], BF16)
        wout = sb(mid, "wout", [128, 8 * 1024], BF16)
        P.nobarrier = {"wg", "wap", "whp", "wout", "wqkv", "whb"}
        attnTv = attnT[:, :].rearrange("p (k n) -> p k n", k=4)
        with contextlib.ExitStack() as ph:
            wqv = wqkv[:, :].rearrange("p (ch n) -> p ch n", ch=8)
            qTz = sb(ph, "qTz", [128, 8 * OWN], BF16)
            qTzv = qTz[:, :].rearrange("p (hp two n) -> p hp two n", two=2, n=OWN)
            qTh = qTz[:, :].rearrange("p (h n) -> p h n", h=8)
            kT = sb(ph, "kT", [128, 2 * EXT], BF16)
            kTv = kT[:, :].rearrange("p (g n) -> p g n", g=2)
            V1 = sb(ph, "V1", [128, 18 * 2 * 65], BF16)
            V1v = V1[:, :].rearrange("p (t g d) -> p t g d", t=18, g=2)
            esb = sb(ph, "esb", [128, 8], F32)
            gq = sb(ph, "gq", [128, 2], F32)
            hTa = [sb(ph, "a_hT%d" % i, [128, 8 * 512], BF16) for i in range(2)]
            sq_ = [sb(ph, "a_sq%d" % i, [128, 640], F32) for i in range(3)]
            st_ = [sb(ph, "a_st%d" % i, [128, 32], F32) for i in range(3)]
            qn_ = [sb(ph, "a_qn%d" % i, [128, 512], BF16) for i in range(2)]
            knd_ = [sb(ph, "a_knd%d" % i, [128, 256], BF16) for i in range(2)]
            pT = [sb(ph, "a_pT%d" % i, [128, 512], BF16) for i in range(6)]
            den_ = [sb(ph, "a_den%d" % i, [128, 8], F32) for i in range(2)]
            on_ = [sb(ph, "a_on%d" % i, [128, 256], BF16) for i in range(2)]
            biasT = sb(ph, "biasT", [128, 5120], BF16)
            for i in range(3):
                P.dma("sync", "biasT", biasT[:, i * 2048:min(5120, (i + 1) * 2048)], din["t_biasT"][:, i * 2048:min(5120, (i + 1) * 2048)], writes=["biasT"])
            P.dma("sync", "esb", esb[:, :], din["sink"].partition_broadcast(128), writes=["esb"])
            P.op("scalar", lambda e: e.activation(out=esb[:, :], in_=esb[:, :], func=AF.Exp), reads=["esb"], writes=["esb"])
            P.dma("sync", "gq", gq[:, 0:1], din["gq"][:, :], writes=["gq"])
            P.dma("sync", "gq", gq[:, 1:2], din["gk"][:, :], writes=["gq"])
            P.op("vector", lambda e: e.tensor_scalar(out=gq[:, 0:1], in0=gq[:, 0:1], scalar1=0.125, scalar2=None, op0=ALU.mult), reads=["gq"], writes=["gq"])
            P.op("vector", lambda e: e.memset(V1[:, :], 1.0), writes=["V1"])

            a_order = list(range(1, 17)) + [0, 17]

            def a1A(j):
                i = a_order[j]
                ck, ci = j // 4, j % 4
                hkey = "a_hT%d" % (ck % 2)
                hv = hTa[ck % 2][:, :].rearrange("p (ch n) -> p ch n", ch=8)
                if ci == 0:
                    if ck < 4:
                        P.dma("sync", hkey, hv, hTg(ck), writes=[hkey])
                    else:
                        P.dma("sync", hkey, hv[:, :, 0:256], hTg(8)[:, :, 0:256], writes=[hkey])
                isq = 1 <= i <= 16
                par = j % 3
                pq, pkv = C.ps[par * 2], C.ps[par * 2 + 1]
                kq, kkv = "ps%d" % (par * 2), "ps%d" % (par * 2 + 1)
                sq, st8 = sq_[par], st_[par]
                if isq:
                    for ch in range(8):
                        mm(P, pq[:, :], hv[:, ch, ci * 128:(ci + 1) * 128], wqv[:, ch, 0:512], ch == 0, ch == 7, [hkey, "wqkv"], [kq])
                for ch in range(8):
                    mm(P, pkv[:, 0:256], hv[:, ch, ci * 128:(ci + 1) * 128], wqv[:, ch, 512:768], ch == 0, ch == 7, [hkey, "wqkv"], [kkv])
                P.op("scalar", lambda e: e.activation(out=sq[:, 512:640], in_=pkv[:, 0:128], func=AF.Square), reads=[kkv], writes=["sqk%d" % par])
                if isq:
                    P.op("scalar", lambda e: e.activation(out=sq[:, 0:512], in_=pq[:, :], func=AF.Square), reads=[kq], writes=["sqq%d" % par])
                    P.op("vector", lambda e: e.tensor_reduce(out=st8[:, 0:10], in_=sq[:, 0:640].rearrange("p (h d) -> p h d", d=64), axis=AX.X, op=ALU.add),
                         reads=["sqq%d" % par, "sqk%d" % par], writes=["ssq%d" % par, "ssk%d" % par])
                    P.op("vector", lambda e: e.tensor_scalar(out=st8[:, 10:20], in0=st8[:, 0:10], scalar1=1.0 / HD, scalar2=EPS, op0=ALU.mult, op1=ALU.add),
                         reads=["ssq%d" % par, "ssk%d" % par], writes=["msq%d" % par, "msk%d" % par])
                    P.op("gpsimd", lambda e: e.tensor_tensor(out=st8[:, 20:30], in0=st8[:, 10:20], in1=C.neghalf[:, 0:1].broadcast_to([128, 10]), op=ALU.pow),
                         reads=["msq%d" % par, "msk%d" % par], writes=["rstdq%d" % par, "rstdk%d" % par])
                else:
                    P.op("vector", lambda e: e.tensor_reduce(out=st8[:, 8:10], in_=sq[:, 512:640].rearrange("p (h d) -> p h d", d=64), axis=AX.X, op=ALU.add),
                         reads=["sqk%d" % par], writes=["ssk%d" % par])
                    rms_rstd(P, st8[:, 8:10], st8[:, 18:20], st8[:, 28:30], C.neghalf[:, 0:1].broadcast_to([128, 2]), float(HD), "k%d" % par)

            def a1B(j):
                i = a_order[j]
                isq = 1 <= i <= 16
                par3 = j % 3
                par = j % 2
                pq, pkv = C.ps[par3 * 2], C.ps[par3 * 2 + 1]
                kq, kkv = "ps%d" % (par3 * 2), "ps%d" % (par3 * 2 + 1)
                st8, qn, knd = st_[par3], qn_[par], knd_[par]
                for dup in range(2):
                    P.op("vector", lambda e, dup=dup: e.tensor_tensor(
                        out=knd[:, :].rearrange("p (g u d) -> p g u d", g=2, u=2)[:, :, dup, :], in0=pkv[:, 0:128].rearrange("p (g d) -> p g d", d=64),
                        in1=st8[:, 28:30].unsqueeze(2).broadcast_to([128, 2, 64]), op=ALU.mult), reads=[kkv, "rstdk%d" % par3], writes=["knd%d" % par])
                P.op("scalar", lambda e: e.activation(out=V1v[:, i, :, 0:64], in_=pkv[:, 128:256].rearrange("p (g d) -> p g d", d=64), func=AF.Copy),
                     reads=[kkv], writes=["V1"])
                if isq:
                    P.op("vector", lambda e: e.tensor_tensor(
                        out=qn[:, :].rearrange("p (h d) -> p h d", d=64), in0=pq[:, :].rearrange("p (h d) -> p h d", d=64),
                        in1=st8[:, 20:28].unsqueeze(2).broadcast_to([128, 8, 64]), op=ALU.mult), reads=[kq, "rstdq%d" % par3], writes=["qn%d" % par])

            def a1C(j):
                i = a_order[j]
                isq = 1 <= i <= 16
                par = j % 2
                qn, knd = qn_[par], knd_[par]
                ptb = C.ps[6][:, :].bitcast(BF16)
                kpt = "ps6"
                for g in range(2):
                    P.op("tensor", lambda e, g=g: e.transpose(out=ptb[:, g * 128:(g + 1) * 128], in_=knd[:, g * 128:(g + 1) * 128], identity=C.ident[:, :]),
                         reads=["knd%d" % par], writes=[kpt])
                P.op("scalar", lambda e: e.activation(out=kTv[:, :, i * 128:(i + 1) * 128], in_=ptb[:, 0:256].rearrange("p (g n) -> p g n", g=2),
                                                      func=AF.Copy, scale=gq[:, 1:2]), reads=[kpt, "gq"], writes=["kT"])
                if isq:
                    ptq = C.ps[7][:, :].bitcast(BF16)
                    kptq = "ps7"
                    for hp in range(4):
                        P.op("tensor", lambda e, hp=hp: e.transpose(out=ptq[:, hp * 128:(hp + 1) * 128], in_=qn[:, hp * 128:(hp + 1) * 128], identity=C.ident[:, :]),
                             reads=["qn%d" % par], writes=[kptq])
                    for two in range(2):
                        zv = qTzv[(1 - two) * 64:(2 - two) * 64, :, two, (i - 1) * 128:i * 128]
                        P.op("gpsimd", lambda e, zv=zv: e.memset(zv, 0.0), writes=["qT"])
                    for two in range(2):
                        eng = "scalar" if two == 0 else "vector"
                        src = ptq[two * 64:(two + 1) * 64, 0:512].rearrange("p (k n) -> p k n", k=4)
                        dstv = qTzv[two * 64:(two + 1) * 64, :, two, (i - 1) * 128:i * 128]
                        if eng == "scalar":
                            P.op("scalar", lambda e, src=src, dstv=dstv, two=two: e.activation(out=dstv, in_=src, func=AF.Copy, scale=gq[two * 64:(two + 1) * 64, 0:1]),
                                 reads=[kptq, "gq"], writes=["qT"])
                        else:
                            P.op("vector", lambda e, src=src, dstv=dstv, two=two: e.tensor_scalar(out=dstv, in0=src, scalar1=gq[two * 64:(two + 1) * 64, 0:1], scalar2=None, op0=ALU.mult),
                                 reads=[kptq, "gq"], writes=["qT"])

            a1A(0)
            a1A(1)
            for i in range(18):
                if i + 2 < 18:
                    a1A(i + 2)
                a1B(i)
                a1C(i)
            wload(P, wg[:, :], din["w_g"], 8, 2048, "wg")
            wload(P, wap[:, :], din["w_ap"], 4, 1024, "wap")
            wload(P, whp[:, :], din["w_hp"], 4, 1024, "whp")
            wload(P, wout[:, :], din["w_out"], 8, 1024, "wout")

            def a2S(it):
                n, g = it // 2, it % 2
                for reli, rel in enumerate((-1, 0, 1)):
                    kt = n + 1 + rel
                    kind = reli
                    if n == 0 and rel == -1:
                        kind = 3
                    if n == 15 and rel == 1:
                        kind = 4
                    bank = (it % 2) * 3 + reli
                    key = "ps%d" % bank
                    mm(P, C.ps[bank][:, :], C.ident[:, :], biasT[:, (kind * 8 + 4 * g) * 128:(kind * 8 + 4 * g + 4) * 128],
                       True, False, ["biasT", "identb"], [key])
                    for r in range(4):
                        h = 4 * g + r
                        mm(P, C.ps[bank][:, r * 128:(r + 1) * 128], kTv[:, g, kt * 128:(kt + 1) * 128], qTh[:, h, n * 128:(n + 1) * 128], False, r == 3, ["kT", "qT"], [key])
                    pt = pT[(it % 2) * 3 + reli]
                    pk = "pT%d" % ((it % 2) * 3 + reli)
                    P.op("scalar", lambda e, pt=pt, bank=bank: e.activation(out=pt[:, :], in_=C.ps[bank][:, :], func=AF.Exp), reads=[key], writes=[pk])

            def a2PV(it):
                n, g = it // 2, it % 2
                po = C.ps[6]
                den, on = den_[it % 2], on_[it % 2]
                for r in range(4):
                    for reli in range(3):
                        kt = n + reli
                        pt = pT[(it % 2) * 3 + reli]
                        pk = "pT%d" % ((it % 2) * 3 + reli)
                        mm(P, po[:, r * 65:(r + 1) * 65], pt[:, r * 128:(r + 1) * 128], V1v[:, kt, g, :], reli == 0, reli == 2, [pk, "V1"], ["ps6"])
                pov = po[:, 0:260].rearrange("p (r d) -> p r d", d=65)
                P.op("vector", lambda e: e.tensor_tensor(out=den[:, 0:4], in0=pov[:, :, 64], in1=esb[:, 4 * g:4 * g + 4], op=ALU.add),
                     reads=["ps6", "esb"], writes=["den%d" % (it % 2)])
                P.op("vector", lambda e: e.reciprocal(out=den[:, 4:8], in_=den[:, 0:4]), reads=["den%d" % (it % 2)], writes=["rden%d" % (it % 2)])
                P.op("vector", lambda e: e.tensor_tensor(out=on[:, :].rearrange("p (r d) -> p r d", d=64), in0=pov[:, :, 0:64],
                                                         in1=den[:, 4:8].unsqueeze(2).broadcast_to([128, 4, 64]), op=ALU.mult),
                     reads=["ps6", "rden%d" % (it % 2)], writes=["on%d" % (it % 2)])

            def a2T(it):
                n, g = it // 2, it % 2
                on = on_[it % 2]
                ptb = C.ps[7][:, :].bitcast(BF16)
                for j in range(2):
                    P.op("tensor", lambda e, j=j: e.transpose(out=ptb[:, j * 128:(j + 1) * 128], in_=on[:, j * 128:(j + 1) * 128], identity=C.ident[:, :]),
                         reads=["on%d" % (it % 2)], writes=["ps7"])
                P.op("scalar", lambda e: e.activation(out=attnTv[:, 2 * g:2 * g + 2, n * 128:(n + 1) * 128],
                                                      in_=ptb[:, 0:256].rearrange("p (k n) -> p k n", k=2), func=AF.Copy),
                     reads=["ps7"], writes=["attnT"])

            a2S(0)
            for it in range(32):
                if it + 1 < 32:
                    a2S(it + 1)
                a2PV(it)
                if it >= 1:
                    a2T(it - 1)
            a2T(31)
            if dbg and "attn" in dbg:
                t2 = sb(ph, "dbg_a", [128, OWN], F32)
                for k in range(4):
                    P.op("vector", lambda e, k=k: e.tensor_copy(out=t2[:, :], in_=attnTv[:, k, :]), reads=["attnT", "t2"], writes=["t2"])
                    P.dma("sync", "dbgs", dbg_d["attn"][k * 128:(k + 1) * 128, :], t2[:, :], reads=["t2"], writes=["dbgo"])
            flush(P)

        with contextlib.ExitStack() as ph:
            wgv = wg[:, :].rearrange("p (ch n) -> p ch n", ch=8)
            wapv = wap[:, :].rearrange("p (ch n) -> p ch n", ch=4)
            whpv = whp[:, :].rearrange("p (ch n) -> p ch n", ch=4)
            woutv = wout[:, :].rearrange("p (ch n) -> p ch n", ch=8)
            hTm = [sb(ph, "m_hT%d" % i, [128, 8 * 512], BF16) for i in range(2)]
            sa_ = [sb(ph, "m_sa%d" % i, [128, 512], F32) for i in range(2)]
            sh_ = [sb(ph, "m_sh%d" % i, [128, 512], F32) for i in range(2)]
            t1_ = [sb(ph, "m_t1%d" % i, [128, 512], F32) for i in range(2)]
            t2_ = [sb(ph, "m_t2%d" % i, [128, 512], F32) for i in range(2)]
            mixT = sb(ph, "m_mix", [128, 8 * 512], BF16)
            mixv = mixT[:, :].rearrange("p (ch n) -> p ch n", ch=8)
            xt = [sb(ph, "m_xt%d" % i, [128, D], F32) for i in range(2)]
            x1t = [sb(ph, "m_x1t%d" % i, [128, D], F32) for i in range(2)]
            xn2 = [sb(ph, "m_xn%d" % i, [128, D], F32) for i in range(2)]
            h2s = [sb(ph, "m_h2s%d" % i, [128, 8 * 512], BF16) for i in range(2)]
            ss = sb(ph, "m_ss", [128, 16], F32)
            ms = sb(ph, "m_ms", [128, 16], F32)
            rstd = sb(ph, "m_rstd", [128, 16], F32)
            for tt in range(4):
                hb_ = hTm[tt % 2]
                hkey = "m_hT%d" % (tt % 2)
                hv = hb_[:, :].rearrange("p (ch n) -> p ch n", ch=8)
                P.dma("sync", hkey, hv, hTg(tt), writes=[hkey])
                tsl = slice(tt * 512, (tt + 1) * 512)
                for m in range(8):
                    msl = slice(m * 128, (m + 1) * 128)
                    pb = (m % 2) * 4
                    sa, sh, t1, t2m = sa_[m % 2], sh_[m % 2], t1_[m % 2], t2_[m % 2]
                    ksa, ksh, kt1, kt2 = "sa%d" % (m % 2), "sh%d" % (m % 2), "t1%d" % (m % 2), "t2%d" % (m % 2)
                    kb = ["ps%d" % (pb + j) for j in range(4)]
                    for ch in range(8):
                        mm(P, C.ps[pb][:, :], wgv[:, ch, msl], hv[:, ch, :], ch == 0, ch == 7, ["wg", hkey], [kb[0]])
                    P.op("scalar", lambda e, sa=sa, pb=pb: e.activation(out=sa[:, :], in_=C.ps[pb][:, :], func=AF.Sigmoid), reads=[kb[0]], writes=[ksa])
                    for ch in range(8):
                        mm(P, C.ps[pb + 1][:, :], wgv[:, ch, 1024 + m * 128:1024 + (m + 1) * 128], hv[:, ch, :], ch == 0, ch == 7, ["wg", hkey], [kb[1]])
                    P.op("scalar", lambda e, sh=sh, pb=pb: e.activation(out=sh[:, :], in_=C.ps[pb + 1][:, :], func=AF.Sigmoid), reads=[kb[1]], writes=[ksh])
                    for k in range(4):
                        mm(P, C.ps[pb + 2][:, :], wapv[:, k, msl], attnTv[:, k, tsl], k == 0, k == 3, ["wap", "attnT"], [kb[2]])
                    for k in range(4):
                        mm(P, C.ps[pb + 3][:, :], whpv[:, k, msl], hyTv[:, k, tsl], k == 0, k == 3, ["whp", "hyT"], [kb[3]])
                    P.op("vector", lambda e, t1=t1, sa=sa, pb=pb: e.tensor_tensor(out=t1[:, :], in0=C.ps[pb + 2][:, :], in1=sa[:, :], op=ALU.mult), reads=[kb[2], ksa], writes=[kt1])
                    P.op("vector", lambda e, t2m=t2m, sh=sh, pb=pb: e.tensor_tensor(out=t2m[:, :], in0=C.ps[pb + 3][:, :], in1=sh[:, :], op=ALU.mult), reads=[kb[3], ksh], writes=[kt2])
                    P.op("gpsimd", lambda e, m=m, t1=t1, t2m=t2m: e.tensor_tensor(out=mixv[:, m, :], in0=t1[:, :], in1=t2m[:, :], op=ALU.add), reads=[kt1, kt2], writes=["mix"])
                h2b = h2s[tt % 2]
                h2key = "h2s%d" % (tt % 2)

                def mA(s, tt=tt):
                    idx = tt * 4 + s
                    xb, xkey = xt[idx % 2], "m_xt%d" % (idx % 2)
                    x1b_, x1key = x1t[idx % 2], "m_x1t%d" % (idx % 2)
                    r0 = idx * 128
                    P.dma("sync", xkey, xb[:, :], din["x_in"][r0:r0 + 128, :], writes=[xkey])
                    for nh in range(2):
                        bank = 4 + nh
                        key = "ps%d" % bank
                        for m in range(8):
                            mm(P, C.ps[bank][:, :], mixv[:, m, s * 128:(s + 1) * 128], woutv[:, m, nh * 512:(nh + 1) * 512], m == 0, m == 7, ["mix", "wout"], [key])
                        P.op("vector", lambda e, nh=nh, bank=bank, xb=xb, x1b_=x1b_: e.tensor_tensor(
                            out=x1b_[:, nh * 512:(nh + 1) * 512], in0=C.ps[bank][:, :], in1=xb[:, nh * 512:(nh + 1) * 512], op=ALU.add),
                            reads=[key, xkey], writes=[x1key])
                    P.dma("gpsimd", "x1st%d" % (idx % 2), x1_d[idx * 128:(idx + 1) * 128, :], x1b_[:, :], reads=[x1key], writes=["x1d"])
                    norm_stats(P, C, x1b_, x1key, xn2[idx % 2], ss, ms, rstd, idx, "m_xn%d" % (idx % 2), junkkey="m_xn%d" % (idx % 2))

                def mB(s, tt=tt, h2b=h2b, h2key=h2key):
                    idx = tt * 4 + s
                    dv = h2b[:, :].rearrange("p (ch n) -> p ch n", ch=8)[:, :, s * 128:(s + 1) * 128]
                    norm_finish(P, C, x1t[idx % 2], "m_x1t%d" % (idx % 2), xn2[idx % 2], "m_xn%d" % (idx % 2), rstd, idx, g2, dv, h2key, [6, 7])

                mA(0)
                for s in range(4):
                    if s + 1 < 4:
                        mA(s + 1)
                    mB(s)
                P.dma("gpsimd", "h2st%d" % (tt % 2), hT2v[:, :, tsl], h2b[:, :].rearrange("p (ch n) -> p ch n", ch=8), reads=[h2key], writes=["hT2d"])
            if dbg and "x1" in dbg:
                barrier(P)
                for idx in range(16):
                    P.dma("sync", "dbgl", xt[0][:, :], x1_d[idx * 128:(idx + 1) * 128, :], writes=["m_xt0"])
                    P.dma("sync", "dbgs", dbg_d["x1"][idx * 128:(idx + 1) * 128, :], xt[0][:, :], reads=["m_xt0"], writes=["dbgo"])
            flush(P)

        mid.close()
        with contextlib.ExitStack() as ph:
            actT = sb(ph, "actT", [128, NFC * OWN], BF16)
            actv = actT[:, :].rearrange("p (k n) -> p k n", k=NFC)
            wdn = sb(ph, "wdn", [128, NFC * D], BF16)
            wdnv = wdn[:, :].rearrange("p (k n) -> p k n", k=NFC)
            h2a = sb(ph, "h2a", [128, 8 * OWN], BF16)
            h2v = h2a[:, :].rearrange("p (ch n) -> p ch n", ch=8)
            wgu = [sb(ph, "wgu%d" % i, [128, 8 * 256], BF16) for i in range(2)]
            sg = [sb(ph, "f_sg%d" % i, [128, 512], F32) for i in range(2)]
            xr = [sb(ph, "f_xr%d" % i, [128, D], F32) for i in range(2)]
            ot = [sb(ph, "f_ot%d" % i, [128, D], F32) for i in range(2)]
            for i in range(4):
                P.dma("sync", "h2a%d" % i, h2v[:, :, i * 512:(i + 1) * 512], hT2v[:, :, i * 512:(i + 1) * 512], writes=["h2a%d" % i])
            for k in range(NFC):
                wb = wgu[k % 2]
                wkey = "wgu%d" % (k % 2)
                wv = wb[:, :].rearrange("p (ch n) -> p ch n", ch=8)
                P.dma("gpsimd", wkey, wv[:, :, 0:128], din["w_gate"].rearrange("(ch p) n -> p ch n", p=128)[:, :, k * 128:(k + 1) * 128], writes=[wkey])
                P.dma("gpsimd", wkey, wv[:, :, 128:256], din["w_up"].rearrange("(ch p) n -> p ch n", p=128)[:, :, k * 128:(k + 1) * 128], writes=[wkey])
                if k == 1:
                    wload(P, wdn[:, :], din["w_down"], NFC, D, "wdn")
                for tt in range(4):
                    it = k * 4 + tt
                    bg, bu = (it % 2) * 2, (it % 2) * 2 + 1
                    for ch in range(8):
                        mm(P, C.ps[bg][:, :], wv[:, ch, 0:128], h2v[:, ch, tt * 512:(tt + 1) * 512], ch == 0, ch == 7, [wkey, "h2a%d" % tt], ["ps%d" % bg])
                    for ch in range(8):
                        mm(P, C.ps[bu][:, :], wv[:, ch, 128:256], h2v[:, ch, tt * 512:(tt + 1) * 512], ch == 0, ch == 7, [wkey, "h2a%d" % tt], ["ps%d" % bu])
                    sgt, sgk = sg[it % 2], "sg%d" % (it % 2)
                    P.op("scalar", lambda e, sgt=sgt, bg=bg: e.activation(out=sgt[:, :], in_=C.ps[bg][:, :], func=AF.Silu), reads=["ps%d" % bg], writes=[sgk])
                    P.op("vector", lambda e, sgt=sgt, bu=bu, k=k, tt=tt: e.tensor_tensor(
                        out=actv[:, k, tt * 512:(tt + 1) * 512], in0=C.ps[bu][:, :], in1=sgt[:, :], op=ALU.mult), reads=["ps%d" % bu, sgk], writes=["actT"])
            for s in range(16):
                xb, xkey = xr[s % 2], "f_xr%d" % (s % 2)
                ob, okey = ot[s % 2], "f_ot%d" % (s % 2)
                P.dma("sync", xkey, xb[:, :], x1_d[s * 128:(s + 1) * 128, :], writes=[xkey])
                for nh in range(2):
                    bank = 4 + (s % 2) * 2 + nh
                    key = "ps%d" % bank
                    for k in range(NFC):
                        mm(P, C.ps[bank][:, :], actv[:, k, s * 128:(s + 1) * 128], wdnv[:, k, nh * 512:(nh + 1) * 512], k == 0, k == NFC - 1, ["actT", "wdn"], [key])
                    P.op("vector", lambda e, nh=nh, bank=bank, xb=xb, ob=ob: e.tensor_tensor(
                        out=ob[:, nh * 512:(nh + 1) * 512], in0=C.ps[bank][:, :], in1=xb[:, nh * 512:(nh + 1) * 512], op=ALU.add),
                        reads=[key, xkey], writes=[okey])
                P.dma("sync", "ost%d" % (s % 2), out_d[s * 128:(s + 1) * 128, :], ob[:, :], reads=[okey], writes=["outd"])
            flush(P)
    return nc


_CACHE = {}


def kernel(**inputs):
    ins = [host_inputs(inputs, c) for c in range(8)]
    shapes = {k: v.shape for k, v in ins[0].items()}
    if "nc" not in _CACHE:
        _CACHE["nc"] = build_program(shapes)
    res = run_bass_kernel_spmd(_CACHE["nc"], ins, core_ids=list(range(8)))
    x = np.asarray(inputs["x"])
    out = np.zeros(x.shape, np.float32)
    for c in range(8):
        b, par = c // 2, c % 2
        out[b, par * OWN:(par + 1) * OWN] = res.results[c]["out"]
    return out
```
